# Optimizing a Trainium2 kernel written in Bass

```python
import jax, jax.numpy as jnp
from jax import lax
import numpy as np

D_MODEL = 1024
BATCH = 16
SEQ = 4096
DEPTH = 4

GRID_W = 64
CTX_LEN = 256
HEAD_DIM = 64
N_Q_HEADS = 8
N_KV_HEADS = 2
Q_PER_KV = N_Q_HEADS // N_KV_HEADS
ATTN_WIDTH = N_Q_HEADS * HEAD_DIM
KV_WIDTH = N_KV_HEADS * HEAD_DIM
CONV_WIDTH = D_MODEL - ATTN_WIDTH
WINDOW = 128
BLOCK = 128
ROPE_THETA = 10000.0
N_FOURIER_GROUPS = 4
FOURIER_GROUP = D_MODEL // N_FOURIER_GROUPS
D_FF = -(-(-(-8 * D_MODEL // 3)) // 256) * 256
N_EVEN = (DEPTH + 1) // 2
N_ODD = DEPTH // 2
B_START = 0
C_START = CONV_WIDTH
H_START = 2 * CONV_WIDTH
Q_START = 3 * CONV_WIDTH
K_START = Q_START + ATTN_WIDTH
V_START = K_START + KV_WIDTH
IN_WIDTH = V_START + KV_WIDTH
EPS = 1e-6

kernel_name = "hybrid_conv_swa_fourier_dit_block"


def rmsnorm(x, g):
    xf = x.astype(jnp.float32)
    y = xf * lax.rsqrt(jnp.mean(xf * xf, axis=-1, keepdims=True) + EPS)
    return (y * g.astype(jnp.float32)).astype(x.dtype)


def modulate(h, shift, scale):
    return h * (1 + scale) + shift


def adaln(cond, w, bias):
    return jax.nn.silu(cond) @ w + bias


def swiglu(h, wg, wu, wd):
    return (jax.nn.silu(h @ wg) * (h @ wu)) @ wd


def axial_rope_tables(n, dtype):
    rows = n // GRID_W
    r = jnp.repeat(jnp.arange(rows), GRID_W).astype(jnp.float32)
    col = jnp.tile(jnp.arange(GRID_W), rows).astype(jnp.float32)
    quarter = HEAD_DIM // 4
    inv = ROPE_THETA ** (-jnp.arange(quarter, dtype=jnp.float32) / quarter)
    ang_r = (r[:, None] * inv)[:, None, :]
    ang_c = (col[:, None] * inv)[:, None, :]
    return (jnp.cos(ang_r).astype(dtype), jnp.sin(ang_r).astype(dtype),
            jnp.cos(ang_c).astype(dtype), jnp.sin(ang_c).astype(dtype))


def _rope_half(x, cos, sin):
    h = x.shape[-1] // 2
    x1, x2 = x[..., :h], x[..., h:]
    return jnp.concatenate([x1 * cos - x2 * sin, x2 * cos + x1 * sin], axis=-1)


def apply_axial_rope(x, tabs):
    cr, sr, cc, sc = tabs
    half = HEAD_DIM // 2
    return jnp.concatenate([_rope_half(x[..., :half], cr, sr),
                            _rope_half(x[..., half:], cc, sc)], axis=-1)


def short_conv(u, w):
    n = u.shape[1]
    up = jnp.pad(u, ((0, 0), (1, 1), (0, 0)))
    return up[:, :n] * w[0] + up[:, 1:n + 1] * w[1] + up[:, 2:] * w[2]


def gated_conv(p, conv_w):
    bg = p[..., B_START:C_START]
    cg = p[..., C_START:H_START]
    hv = p[..., H_START:Q_START]
    return bg * short_conv(cg * hv, conv_w)


def window_attention(q, k, v, kc, vc, sink):
    b, n = q.shape[:2]
    nb = n // BLOCK
    scale = HEAD_DIM ** -0.5
    qb = q.reshape(b, nb, BLOCK, N_KV_HEADS, Q_PER_KV, HEAD_DIM)

    def band(t):
        tp = jnp.pad(t, ((0, 0), (BLOCK, BLOCK), (0, 0), (0, 0)))
        tp = tp.reshape(b, nb + 2, BLOCK, N_KV_HEADS, HEAD_DIM)
        return jnp.concatenate([tp[:, :-2], tp[:, 1:-1], tp[:, 2:]], axis=2)

    kb, vb = band(k), band(v)
    qpos = jnp.arange(n).reshape(nb, BLOCK)
    kpos = jnp.arange(nb)[:, None] * BLOCK - BLOCK + jnp.arange(3 * BLOCK)[None, :]
    dist = qpos[:, :, None] - kpos[:, None, :]
    valid = (jnp.abs(dist) <= WINDOW) & (kpos[:, None, :] >= 0) & (kpos[:, None, :] < n)
    s_loc = jnp.einsum('bnqhgd,bnkhd->bnhgqk', qb, kb).astype(jnp.float32) * scale
    s_loc = jnp.where(valid[None, :, None, None], s_loc, -jnp.inf)
    s_ctx = jnp.einsum('bnqhgd,bchd->bnhgqc', qb, kc).astype(jnp.float32) * scale
    s_sink = jnp.broadcast_to(sink.astype(jnp.float32).reshape(N_KV_HEADS, Q_PER_KV)[None, None, :, :, None, None],
                              s_loc.shape[:-1] + (1,))
    probs = jax.nn.softmax(jnp.concatenate([s_loc, s_ctx, s_sink], axis=-1), axis=-1).astype(v.dtype)
    n_loc = 3 * BLOCK
    n_ctx = kc.shape[1]
    out = (jnp.einsum('bnhgqk,bnkhd->bnqhgd', probs[..., :n_loc], vb)
           + jnp.einsum('bnhgqc,bchd->bnqhgd', probs[..., n_loc:n_loc + n_ctx], vc))
    return out.reshape(b, n, ATTN_WIDTH)


def context_attention(q, k, v, sink):
    b, L = q.shape[:2]
    scale = HEAD_DIM ** -0.5
    qg = q.reshape(b, L, N_KV_HEADS, Q_PER_KV, HEAD_DIM)
    s = jnp.einsum('blhgd,bchd->bhglc', qg, k).astype(jnp.float32) * scale
    s_sink = jnp.broadcast_to(sink.astype(jnp.float32).reshape(N_KV_HEADS, Q_PER_KV)[None, :, :, None, None],
                              s.shape[:-1] + (1,))
    probs = jax.nn.softmax(jnp.concatenate([s, s_sink], axis=-1), axis=-1).astype(v.dtype)
    out = jnp.einsum('bhglc,bchd->blhgd', probs[..., :-1], v)
    return out.reshape(b, L, ATTN_WIDTH)


def fourier_mix(h):
    b, n, _ = h.shape
    hg = h.astype(jnp.float32).reshape(b, n, N_FOURIER_GROUPS, FOURIER_GROUP)
    f = jnp.fft.fft2(hg, axes=(1, 3), norm='ortho').real
    return f.reshape(b, n, D_MODEL).astype(h.dtype)


def setup_inputs(seed: int = 0) -> dict:
    key = jax.random.key(seed)
    ks = jax.random.split(key, 20)
    f32 = jnp.float32
    d = D_MODEL
    return {
        'x': jax.random.normal(ks[0], (BATCH, SEQ, d), f32),
        'c': jax.random.normal(ks[1], (BATCH, d), f32),
        'ctx': jax.random.normal(ks[2], (BATCH, CTX_LEN, d), f32),
        'c_ctx': jax.random.normal(ks[3], (d,), f32),
        'w_ada': jax.random.normal(ks[4], (DEPTH, d, 6 * d), f32) * (0.5 * d ** -0.5),
        'b_ada': jax.random.normal(ks[5], (DEPTH, 6 * d), f32) * 0.01,
        'norm1_g': 1.0 + 0.01 * jax.random.normal(ks[6], (DEPTH, d), f32),
        'norm2_g': 1.0 + 0.01 * jax.random.normal(ks[7], (DEPTH, d), f32),
        'w_in': jax.random.normal(ks[8], (N_EVEN, d, IN_WIDTH), f32) * d ** -0.5,
        'conv_w': jax.random.normal(ks[9], (N_EVEN, 3, CONV_WIDTH), f32) * 3 ** -0.5,
        'sink': jax.random.normal(ks[10], (N_EVEN, N_Q_HEADS), f32),
        'w_mix_out': jax.random.normal(ks[11], (N_EVEN, CONV_WIDTH + ATTN_WIDTH, d), f32) * (CONV_WIDTH + ATTN_WIDTH) ** -0.5,
        'w_fourier_out': jax.random.normal(ks[12], (N_ODD, d, d), f32) * d ** -0.5,
        'w_ffn_gate': jax.random.normal(ks[13], (DEPTH, d, D_FF), f32) * d ** -0.5,
        'w_ffn_up': jax.random.normal(ks[14], (DEPTH, d, D_FF), f32) * d ** -0.5,
        'w_ffn_down': jax.random.normal(ks[15], (DEPTH, D_FF, d), f32) * D_FF ** -0.5,
        'final_g': 1.0 + 0.01 * jax.random.normal(ks[16], (d,), f32),
    }


def reference(x, c, ctx, c_ctx, w_ada, b_ada, norm1_g, norm2_g, w_in, conv_w, sink,
              w_mix_out, w_fourier_out, w_ffn_gate, w_ffn_up, w_ffn_down, final_g):
    b, n, _ = x.shape
    L = ctx.shape[1]
    rope_tabs = axial_rope_tables(n, x.dtype)
    xc = ctx
    for l in range(DEPTH):
        ctx_after = any(j % 2 == 0 for j in range(l + 1, DEPTH))
        ctx_here = (l % 2 == 0) or ctx_after
        sh1, sc1, g1, sh2, sc2, g2 = jnp.split(adaln(c, w_ada[l], b_ada[l])[:, None, :], 6, axis=-1)
        h = modulate(rmsnorm(x, norm1_g[l]), sh1, sc1)
        if ctx_here:
            csh1, csc1, cg1, csh2, csc2, cg2 = jnp.split(adaln(c_ctx, w_ada[l], b_ada[l]), 6, axis=-1)
            hc = modulate(rmsnorm(xc, norm1_g[l]), csh1, csc1)
        if l % 2 == 0:
            e = l // 2
            if ctx_after:
                pc = hc @ w_in[e]
                kvc = pc[..., K_START:]
            else:
                kvc = hc @ w_in[e][:, K_START:]
            kc = kvc[..., :KV_WIDTH].reshape(b, L, N_KV_HEADS, HEAD_DIM)
            vc = kvc[..., KV_WIDTH:].reshape(b, L, N_KV_HEADS, HEAD_DIM)
            p = h @ w_in[e]
            a_out = gated_conv(p, conv_w[e])
            q = apply_axial_rope(p[..., Q_START:K_START].reshape(b, n, N_Q_HEADS, HEAD_DIM), rope_tabs)
            k = apply_axial_rope(p[..., K_START:V_START].reshape(b, n, N_KV_HEADS, HEAD_DIM), rope_tabs)
            v = p[..., V_START:].reshape(b, n, N_KV_HEADS, HEAD_DIM)
            b_out = window_attention(q, k, v, kc, vc, sink[e])
            x = x + g1 * (jnp.concatenate([a_out, b_out], axis=-1) @ w_mix_out[e])
            if ctx_after:
                a_c = gated_conv(pc, conv_w[e])
                qc = pc[..., Q_START:K_START].reshape(b, L, N_Q_HEADS, HEAD_DIM)
                b_c = context_attention(qc, kc, vc, sink[e])
                xc = xc + cg1 * (jnp.concatenate([a_c, b_c], axis=-1) @ w_mix_out[e])
        else:
            o = l // 2
            x = x + g1 * (fourier_mix(h) @ w_fourier_out[o])
            if ctx_after:
                xc = xc + cg1 * (fourier_mix(hc) @ w_fourier_out[o])
        h2 = modulate(rmsnorm(x, norm2_g[l]), sh2, sc2)
        x = x + g2 * swiglu(h2, w_ffn_gate[l], w_ffn_up[l], w_ffn_down[l])
        if ctx_after:
            hc2 = modulate(rmsnorm(xc, norm2_g[l]), csh2, csc2)
            xc = xc + cg2 * swiglu(hc2, w_ffn_gate[l], w_ffn_up[l], w_ffn_down[l])
    return rmsnorm(x, final_g)
```

```python
import math
from contextlib import ExitStack

import numpy as np
import ml_dtypes

import concourse.bass as bass
import concourse.mybir as mybir
from concourse.bass_utils import run_bass_kernel_spmd

F32 = mybir.dt.float32
BF16 = mybir.dt.bfloat16
ALU = mybir.AluOpType
AF = mybir.ActivationFunctionType

N_CORES = 8
D = 1024
KC = 8
SEQ = 4096
NT = 512
NTILE = SEQ // NT
CTX = 256
DFF = 2816
FC = DFF // 128
DEPTH = 4
EPS = 1e-6
WIN_COLS = 23 * 128
SAME_ENGINE_SYNC = True

ENGS = ("pe", "act", "dve", "pool", "sp")
SB_BASE = 16512
SB_END = 229376


def _dtsize(dt):
    return 2 if dt == BF16 else 4


class Res:
    __slots__ = ("name", "w", "r")

    def __init__(self, name=""):
        self.name = name
        self.w = None
        self.r = {}


def RL(n, name=""):
    return [Res(f"{name}{i}") for i in range(n)]


class Op:
    __slots__ = ("fn", "deps", "dma", "sig", "signo")

    def __init__(self, fn, deps, dma):
        self.fn = fn
        self.deps = deps
        self.dma = dma
        self.sig = False
        self.signo = 0


class Prog:
    def __init__(self, nc):
        self.nc = nc
        self.ops = {e: [] for e in ENGS}
        self.dma_cnt = {}
        self.sb_ptr = SB_BASE
        self.phase_base = None
        self.uid = 0

    def alloc(self, name, shape, dtype):
        nbytes = int(np.prod(shape[1:])) * _dtsize(dtype)
        off = (self.sb_ptr + 31) // 32 * 32
        self.sb_ptr = off + nbytes
        assert self.sb_ptr <= SB_END, f"SBUF overflow allocating {name}: {self.sb_ptr}"
        self.uid += 1
        h = self.nc.alloc_sbuf_tensor_at(f"{name}_{self.uid}", list(shape), dtype, offset=off)
        return h.ap()

    def end_persistent(self):
        self.phase_base = self.sb_ptr

    def phase_begin(self):
        self.barrier()
        self.sb_ptr = self.phase_base

    def op(self, eng, fn, reads=(), writes=(), dma=None):
        deps = set()
        for r in reads:
            if r.w is not None:
                deps.add(r.w)
        for w in writes:
            if w.w is not None:
                deps.add(w.w)
            deps.update(w.r.values())
        rec = Op(fn, deps, dma)
        idx = len(self.ops[eng])
        self.ops[eng].append(rec)
        if dma is not None:
            self.dma_cnt[dma] = self.dma_cnt.get(dma, 0) + 16
            ev = ("D", dma, self.dma_cnt[dma])
            rkey = ("D", dma)
        else:
            ev = ("E", eng, idx)
            rkey = ("E", eng)
        for r in reads:
            r.r[rkey] = ev
        for w in writes:
            w.w = ev
            w.r = {}
        return rec

    def barrier(self):
        deps = set()
        for e in ENGS:
            for i in range(len(self.ops[e]) - 1, -1, -1):
                if self.ops[e][i].fn is not None and self.ops[e][i].dma is None:
                    deps.add(("E", e, i))
                    break
        for k, c in self.dma_cnt.items():
            deps.add(("D", k, c))
        for e in ENGS:
            self.ops[e].append(Op(None, set(deps), None))

    def mm(self, out, lhsT, rhs, start, stop, reads, writes):
        return self.op("pe", lambda t: t.matmul(out, lhsT, rhs, start=start, stop=stop), reads, writes)

    def dma(self, eng, out, in_, reads, writes, key):
        return self.op(eng, lambda q: q.dma_start(out=out, in_=in_), reads, writes, dma=key)

    def dma_batch(self, eng, items, key):
        for (out, in_, reads, writes) in items:
            self.dma(eng, out, in_, reads, writes, key)
        ev = ("D", key, self.dma_cnt[key])
        for (_, _, reads, writes) in items:
            for r in reads:
                r.r[("D", key)] = ev
            for w in writes:
                w.w = ev

    def act(self, out, in_, func, reads, writes, bias=None, scale=None, eng="act"):
        kw = {}
        if bias is not None:
            kw["bias"] = bias
        if scale is not None:
            kw["scale"] = scale
        return self.op(eng, lambda a: a.activation(out, in_, func, **kw), reads, writes)

    def tt(self, eng, out, in0, in1, op, reads, writes):
        return self.op(eng, lambda v: v.tensor_tensor(out, in0, in1, op), reads, writes)

    def ts(self, eng, out, in0, s1, s2, op0, op1, reads, writes):
        if s2 is None:
            return self.op(eng, lambda v: v.tensor_scalar(out, in0, s1, None, op0), reads, writes)
        return self.op(eng, lambda v: v.tensor_scalar(out, in0, s1, s2, op0, op1), reads, writes)

    def stt(self, eng, out, in0, scalar, in1, op0, op1, reads, writes):
        return self.op(eng, lambda v: v.scalar_tensor_tensor(out, in0, scalar, in1, op0, op1), reads, writes)

    def copy(self, eng, out, in_, reads, writes):
        if eng == "act":
            return self.op(eng, lambda a: a.copy(out, in_), reads, writes)
        return self.op(eng, lambda v: v.tensor_copy(out, in_), reads, writes)

    def memset(self, eng, ap, val, writes):
        return self.op(eng, lambda v: v.memset(ap, val), (), writes)

    def emit(self):
        nc = self.nc
        for e in ENGS:
            for op in self.ops[e]:
                for d in op.deps:
                    if d[0] == "E":
                        if d[1] == e and (e == "pe" or e == "sp" or not SAME_ENGINE_SYNC):
                            continue
                        self.ops[d[1]][d[2]].sig = True
        for e in ENGS:
            n = 0
            for op in self.ops[e]:
                if op.sig:
                    if op.fn is None:
                        op.sig = False
                        continue
                    n += 1
                    op.signo = n
        with ExitStack() as st:
            esem = {e: st.enter_context(nc.semaphore(f"s_{e}")) for e in ENGS}
            dsem = {k: st.enter_context(nc.semaphore(f"d_{i}")) for i, k in enumerate(self.dma_cnt)}
            block = st.enter_context(nc.Block())

            def run(e, eng):
                known = {}
                for op in self.ops[e]:
                    waits = {}
                    for d in op.deps:
                        if d[0] == "E":
                            if d[1] == e and (e == "pe" or e == "sp" or not SAME_ENGINE_SYNC):
                                continue
                            key = ("E", d[1])
                            val = self.ops[d[1]][d[2]].signo
                            sem = esem[d[1]]
                        else:
                            key = ("D", d[1])
                            val = d[2]
                            sem = dsem[d[1]]
                        if val <= 0:
                            continue
                        if known.get(key, 0) < val and waits.get(key, (None, 0))[1] < val:
                            waits[key] = (sem, val)
                    for key, (sem, val) in waits.items():
                        eng.wait_ge(sem, val)
                        known[key] = val
                    if op.fn is not None:
                        ins = op.fn(eng)
                        if op.sig:
                            ins.then_inc(esem[e], 1)
                        if op.dma is not None:
                            ins.then_inc(dsem[op.dma], 16)

            @block.tensor
            def _(t):
                run("pe", t)

            @block.scalar
            def _(a):
                run("act", a)

            @block.vector
            def _(v):
                run("dve", v)

            @block.gpsimd
            def _(g):
                run("pool", g)

            @block.sync
            def _(s):
                run("sp", s)


def bcast_mid(ap2d, n):
    a = ap2d.ap
    return bass.AP(ap2d.tensor, ap2d.offset, [list(a[0]), [0, n], list(a[1])])


class Builder:
    def __init__(self, layers=(0, 1, 2, 3), final_norm=True, do_mixer=True, do_ffn=True, dbg=""):
        self.dbg = dbg
        self.layers = tuple(layers)
        self.final_norm = final_norm
        self.do_mixer = do_mixer
        self.do_ffn = do_ffn
        self.nc = bass.Bass("TRN2", target_bir_lowering=False)
        self.pg = Prog(self.nc)
        self._declare_dram()
        self._persistent()

    def _in(self, name, shape, dt=F32):
        return self.nc.dram_tensor(name, list(shape), dt, kind="ExternalInput").ap()

    def _scr(self, name, shape, dt):
        kind = "ExternalOutput" if name in self.dbg.split(",") else "Internal"
        return self.nc.dram_tensor(name, list(shape), dt, kind=kind).ap()

    def _declare_dram(self):
        nc = self.nc
        self.xT = self._in("xT", [2, D, SEQ])
        self.ctxT = self._in("ctxT", [D, 2 * CTX])
        self.cT = self._in("cT", [128, KC, 3])
        self.w_ada = self._in("w_ada", [DEPTH, D, 6 * D])
        self.b_adaT = self._in("b_adaT", [128, DEPTH, 48])
        self.g1T = self._in("g1T", [128, DEPTH, KC])
        self.g2T = self._in("g2T", [128, DEPTH, KC])
        self.gfT = self._in("gfT", [128, KC])
        self.w_in = self._in("w_in_ext", [2, D, WIN_COLS])
        self.convT = self._in("convT", [128, 2, 3, 4])
        self.sinkb = self._in("sinkb", [128, 2, 8])
        self.w_mix = self._in("w_mix_out", [2, D, D])
        self.w_four = self._in("w_fourier_out", [2, D, D])
        self.w_g = self._in("w_ffn_gate", [DEPTH, D, DFF])
        self.w_u = self._in("w_ffn_up", [DEPTH, D, DFF])
        self.w_d = self._in("w_ffn_down", [DEPTH, DFF, D])
        self.cs256 = self._in("cs256", [256, 512], BF16)
        self.csp256 = self._in("csp256", [256, 512], BF16)
        self.cnf = self._in("cnf", [SEQ // 2, SEQ], BF16)
        self.snf = self._in("snf", [SEQ // 2, SEQ], BF16)
        self.ijn = self._in("ijn", [128, 3, 128], BF16)
        self.c0row = self._in("c0row", [1, NT], BF16)
        self.ropeC = self._in("ropeC", [128, SEQ])
        self.ropeS = self._in("ropeS", [128, SEQ])
        self.masks = self._in("masks", [128, 2, 512], BF16)
        self.ident = self._in("ident", [128, 128], BF16)
        self.selrow = self._in("selrow", [1, 128], BF16)
        self.yT = nc.dram_tensor("yT", [2, D, SEQ], F32, kind="ExternalOutput").ap()
        self.xs = self._scr("xs", [2, D, SEQ], F32)
        self.xcs = self._scr("xcs", [D, 2 * CTX], F32)
        self.bgs = self._scr("bgs", [2, 512, SEQ], F32)
        self.us = self._scr("us", [2, 512, SEQ], F32)
        self.qs = self._scr("qs", [2, 512, SEQ], BF16)
        self.kks = self._scr("kks", [2, 128, SEQ], BF16)
        self.vs = self._scr("vs", [2, SEQ, 256], BF16)
        self.bgc = self._scr("bgc", [512, 2 * CTX], F32)
        self.uc = self._scr("uc", [512, 2 * CTX], F32)
        self.qc = self._scr("qc", [512, 2 * CTX], BF16)
        self.kkc = self._scr("kkc", [128, 2 * CTX], BF16)
        self.vc = self._scr("vc", [2 * CTX, 256], BF16)
        self.zs = self._scr("zs", [2, SEQ, 2048], BF16)
        self.zc = self._scr("zc", [2 * CTX, 2048], BF16)
        self.fs = self._scr("fs", [2, D, SEQ], BF16)
        self.fc = self._scr("fc", [D, 2 * CTX], BF16)
        self.r_x = [RL(NTILE, f"x{b}_") for b in range(2)]
        self.r_xc = Res("xc")
        self.r_m1 = [[RL(5, f"m1_{b}_{i}_") for i in range(NTILE)] for b in range(2)]
        self.r_m1c = RL(5, "m1c")
        self.r_z = [RL(NTILE, f"z{b}_") for b in range(2)]
        self.r_zc = Res("zc")
        self.r_f = [[RL(8, f"f{b}_{i}_") for i in range(NTILE)] for b in range(2)]
        self.r_fc = RL(16, "fc")
        self.x_cur = self.xT
        self.xc_cur = self.ctxT

    def _persistent(self):
        pg = self.pg
        self.modT = pg.alloc("modT", [128, DEPTH, 48, 3], F32)
        self.A1 = pg.alloc("A1", [128, DEPTH, KC, 3], F32)
        self.A2 = pg.alloc("A2", [128, DEPTH, KC, 3], F32)
        self.scT = pg.alloc("scT", [128, KC, 3], F32)
        self.badaT_sb = pg.alloc("bada", [128, DEPTH, 48], F32)
        self.g1_sb = pg.alloc("g1", [128, DEPTH, KC], F32)
        self.g2_sb = pg.alloc("g2", [128, DEPTH, KC], F32)
        self.gf_sb = pg.alloc("gf", [128, KC], F32)
        self.ones_bf = pg.alloc("ones", [128, 128], BF16)
        self.conv_sb = pg.alloc("conv", [128, 2, 3, 4], F32)
        self.esink = pg.alloc("esink", [128, 2, 8], F32)
        self.r_const = Res("const")
        self.r_mod = Res("mod")
        pg.end_persistent()
        self.banks = [self.nc.alloc_psum_tensor(f"bank{i}", [128, 512], F32).ap() for i in range(8)]
        self.r_bank = RL(8, "bank")

    def mod_col(self, l, j0, kc, col):
        return self.modT[:, l, j0 * 8 + kc, col:col + 1]

    def load_consts(self):
        pg = self.pg
        c = self.r_const
        for i, (dst, src) in enumerate([
            (self.scT, self.cT), (self.badaT_sb, self.b_adaT), (self.g1_sb, self.g1T), (self.g2_sb, self.g2T),
            (self.gf_sb, self.gfT), (self.conv_sb, self.convT), (self.esink, self.sinkb),
        ]):
            r = Res(f"c{i}")
            pg.dma("sp", dst, src, [], [r], key=("const",))
        pg.memset("dve", self.ones_bf, 1.0 / D, [c])
        pg.barrier()
        pg.act(self.scT, self.scT, AF.Silu, [c], [c])
        pg.act(self.esink, self.esink, AF.Exp, [c], [c])
        pg.barrier()

    def phase_adaln(self):
        pg = self.pg
        pg.phase_begin()
        NP = 8
        PW = 6 * D // NP
        wst = [pg.alloc(f"wada{i}", [128, KC, PW], F32) for i in range(2)]
        wres = RL(2, "wada")
        nb = 0
        for l in self.layers:
            src = self.w_ada[l].rearrange("(kc p) f -> p kc f", p=128)
            for piece in range(NP):
                nb2 = getattr(self, '_npiece', 0)
                self._npiece = nb2 + 1
                buf = nb2 % 2
                pg.dma("sp", wst[buf], src[:, :, piece * PW:(piece + 1) * PW], [], [wres[buf]], key=("wada", buf))
                for j in range(PW // 128):
                    fch = piece * (PW // 128) + j
                    bk = nb % 4
                    nb += 1
                    ps = self.banks[bk][:, 0:3]
                    for kc in range(KC):
                        pg.mm(ps, wst[buf][:, kc, j * 128:(j + 1) * 128], self.scT[:, kc, :], kc == 0, kc == KC - 1,
                              [wres[buf], self.r_const], [self.r_bank[bk]])
                    pg.ts("dve", self.modT[:, l, fch, :], ps, self.badaT_sb[:, l, fch:fch + 1], None, ALU.add, None,
                          [self.r_bank[bk], self.r_const], [self.r_mod])
        pg.barrier()
        for l in self.layers:
            for col in range(3):
                for (A, gsb, j0) in ((self.A1, self.g1_sb, 1), (self.A2, self.g2_sb, 4)):
                    pg.ts("dve", A[:, l, :, col], self.modT[:, l, j0 * 8:(j0 + 1) * 8, col], 1.0, None, ALU.add, None,
                          [self.r_mod], [self.r_mod])
                    pg.tt("dve", A[:, l, :, col], A[:, l, :, col], gsb[:, l, :], ALU.mult, [self.r_mod, self.r_const],
                          [self.r_mod])
        pg.barrier()

    def norm_mod(self, xt, xres, nt, Afn, shfn, h, hres, scratch):
        pg = self.pg
        sq, sqres, rstd, rres, tmp, tres, ssb = scratch
        for kc in range(KC):
            s = kc % 2
            pg.act(sq[s][:, :nt], xt[:, kc, :], AF.Square, [xres[kc]], [sqres[s]])
            pg.mm(self.banks[ssb][:, :nt], self.ones_bf, sq[s][:, :nt], kc == 0, kc == KC - 1,
                  [sqres[s], self.r_const], [self.r_bank[ssb]])
        pg.act(rstd[:, :nt], self.banks[ssb][:, :nt], AF.Sqrt, [self.r_bank[ssb]], [rres], bias=EPS, scale=1.0)
        pg.op("dve", lambda v, o=rstd[:, :nt]: v.reciprocal(o, o), [rres], [rres])
        for kc in range(KC):
            s = kc % 2
            pg.tt("dve", tmp[s][:, :nt], xt[:, kc, :], rstd[:, :nt], ALU.mult, [xres[kc], rres], [tres[s]])
            sh = shfn(kc)
            pg.act(h[:, kc, :], tmp[s][:, :nt], AF.Identity, [tres[s], self.r_mod, self.r_const], [hres[kc]],
                   bias=(sh if sh is not None else 0.0), scale=Afn(kc))

    def norm_scratch(self, ssb, tmp=None, rtmp=None):
        pg = self.pg
        sq = [pg.alloc(f"sq{i}", [128, NT], BF16) for i in range(2)]
        rstd = pg.alloc("rstd", [128, NT], F32)
        if tmp is None:
            tmp = [pg.alloc(f"ntmp{i}", [128, NT], F32) for i in range(2)]
            rtmp = RL(2, "ntmp")
        return (sq, RL(2, "sq"), rstd, Res("rstd"), tmp, rtmp, ssb)

    def main_tiles(self):
        return [dict(b=b, i=i, ctx=False, col=b) for b in range(2) for i in range(NTILE)]

    def x_src(self, t):
        if t["ctx"]:
            return self.xc_cur.rearrange("(kc p) t -> p kc t", p=128), self.r_xc
        return (self.x_cur[t["b"]].rearrange("(kc p) t -> p kc t", p=128)[:, :, t["i"] * NT:(t["i"] + 1) * NT],
                self.r_x[t["b"]][t["i"]])

    def x_dst(self, t, final=False):
        if t["ctx"]:
            return self.xcs.rearrange("(kc p) t -> p kc t", p=128), self.r_xc
        base = self.yT if final else self.xs
        return (base[t["b"]].rearrange("(kc p) t -> p kc t", p=128)[:, :, t["i"] * NT:(t["i"] + 1) * NT],
                self.r_x[t["b"]][t["i"]])

    def phase_ffn(self, l, with_ctx, last):
        pg = self.pg
        pg.phase_begin()
        Wg = pg.alloc("Wg", [128, KC, DFF], BF16)
        Wu = pg.alloc("Wu", [128, KC, DFF], BF16)
        Wd = pg.alloc("Wd", [128, FC, D], BF16)
        rWg, rWu, rWd = RL(2, "Wg"), RL(2, "Wu"), RL(2, "Wd")
        gsrc = self.w_g[l].rearrange("(kc p) f -> p kc f", p=128)
        usrc = self.w_u[l].rearrange("(kc p) f -> p kc f", p=128)
        dsrc = self.w_d[l].rearrange("(fc p) n -> p fc n", p=128)
        for hlf in range(2):
            pg.dma("pool", Wg[:, 4 * hlf:4 * hlf + 4, :], gsrc[:, 4 * hlf:4 * hlf + 4, :], [], [rWg[hlf]], key=("Wg", hlf))
            pg.dma("pool", Wu[:, 4 * hlf:4 * hlf + 4, :], usrc[:, 4 * hlf:4 * hlf + 4, :], [], [rWu[hlf]], key=("Wu", hlf))
        for hlf in range(2):
            pg.dma("pool", Wd[:, 11 * hlf:11 * hlf + 11, :], dsrc[:, 11 * hlf:11 * hlf + 11, :], [], [rWd[hlf]], key=("Wd", hlf))
        xt = [pg.alloc(f"xt{i}", [128, KC, NT], F32) for i in range(2)]
        rxt = [RL(KC, f"xt{i}_") for i in range(2)]
        h = pg.alloc("h", [128, KC, NT], BF16)
        rh = RL(KC, "h")
        a = pg.alloc("a", [128, FC, NT], BF16)
        ra = RL(FC, "a")
        sg = [pg.alloc(f"sg{i}", [128, NT], F32) for i in range(2)]
        rsg = RL(2, "sg")
        scratch = self.norm_scratch(ssb=6, tmp=sg, rtmp=rsg)
        gb, ub, db = (0, 1), (2, 3), (4, 5)
        tiles = self.main_tiles()
        if with_ctx:
            tiles.append(dict(b=0, i=0, ctx=True, col=2))

        def load(idx):
            t = tiles[idx]
            src, r = self.x_src(t)
            pg.dma("sp", xt[idx % 2], src, [r], rxt[idx % 2], key=("xt", idx % 2))

        def norm(idx):
            t = tiles[idx]
            col = t["col"]
            X, rX = xt[idx % 2], rxt[idx % 2]
            self.norm_mod(X, rX, NT, lambda kc: self.A2[:, l, kc, col:col + 1], lambda kc: self.mod_col(l, 3, kc, col),
                          h, rh, scratch)

        def gateup(idx):
            for fc in range(FC):
                s = fc % 2
                for (W, rW, bk) in ((Wg, rWg, gb[s]), (Wu, rWu, ub[s])):
                    for kc in range(KC):
                        pg.mm(self.banks[bk], W[:, kc, fc * 128:(fc + 1) * 128], h[:, kc, :], kc == 0, kc == KC - 1,
                              [rW[kc // 4], rh[kc]], [self.r_bank[bk]])
                pg.act(sg[s], self.banks[gb[s]], AF.Silu, [self.r_bank[gb[s]]], [rsg[s]])
                pg.tt("dve", a[:, fc, :], sg[s], self.banks[ub[s]], ALU.mult, [rsg[s], self.r_bank[ub[s]]], [ra[fc]])

        def down(idx):
            t = tiles[idx]
            col = t["col"]
            X, rX = xt[idx % 2], rxt[idx % 2]
            for oc in range(KC):
                bk = db[oc % 2]
                for fc in range(FC):
                    pg.mm(self.banks[bk], Wd[:, fc, oc * 128:(oc + 1) * 128], a[:, fc, :], fc == 0, fc == FC - 1,
                          [rWd[fc // 11], ra[fc]], [self.r_bank[bk]])
                pg.stt("dve", X[:, oc, :], self.banks[bk], self.mod_col(l, 5, oc, col), X[:, oc, :], ALU.mult, ALU.add,
                       [self.r_bank[bk], rX[oc], self.r_mod], [rX[oc]])
            if last and self.final_norm and not t["ctx"]:
                self.norm_mod(X, rX, NT, lambda kc: self.gf_sb[:, kc:kc + 1], lambda kc: None, X, rX, scratch)

        def store(idx):
            t = tiles[idx]
            dst, r = self.x_dst(t, final=last)
            pg.dma("sp", dst, xt[idx % 2], rxt[idx % 2], [r], key=("xst", idx % 2))

        load(0)
        norm(0)
        for idx in range(len(tiles)):
            if idx + 1 < len(tiles):
                load(idx + 1)
            gateup(idx)
            if idx + 1 < len(tiles):
                norm(idx + 1)
            down(idx)
            store(idx)
        self.x_cur = self.xs
        if with_ctx:
            self.xc_cur = self.xcs


    def phase_m1(self, l, ctx_full):
        pg = self.pg
        e = l // 2
        pg.phase_begin()
        W = pg.alloc("Win", [128, KC, WIN_COLS], BF16)
        rW = RL(4, "Win")
        wsrc = self.w_in[e].rearrange("(kc p) f -> p kc f", p=128)
        for q4 in range(4):
            pg.dma("pool", W[:, 2 * q4:2 * q4 + 2, :], wsrc[:, 2 * q4:2 * q4 + 2, :], [], [rW[q4]], key=("Win", q4))
        xt = [pg.alloc(f"xt{i}", [128, KC, NT], F32) for i in range(2)]
        rxt = [RL(KC, f"xt{i}_") for i in range(2)]
        rcs = [pg.alloc(f"rc{i}", [128, 2, NT], F32) for i in range(2)]
        rrcs = RL(2, "rcs")
        hb = [pg.alloc(f"h{i}", [128, KC, NT], BF16) for i in range(2)]
        rhb = [RL(KC, f"h{i}_") for i in range(2)]
        scratch = self.norm_scratch(ssb=7)
        bgst = [pg.alloc(f"bgst{i}", [128, 4, NT], F32) for i in range(2)]
        ust = [pg.alloc(f"ust{i}", [128, 4, NT], F32) for i in range(2)]
        qst = [pg.alloc(f"qst{i}", [128, 4, NT], BF16) for i in range(2)]
        kkst = [pg.alloc(f"kkst{i}", [128, NT], BF16) for i in range(2)]
        vst = [pg.alloc(f"vst{i}", [128, 4, 256], BF16) for i in range(2)]
        rbg = [RL(4, "bgst") for _ in range(2)]
        rus = [RL(4, "ust") for _ in range(2)]
        rq = [RL(4, "qst") for _ in range(2)]
        rkk = [RL(1, "kkst") for _ in range(2)]
        rv = RL(2, "vst")
        ctmp = [pg.alloc(f"ctmp{i}", [128, NT], F32) for i in range(2)]
        rct = RL(2, "ctmp")
        t1 = [pg.alloc(f"t1_{i}", [128, NT], F32) for i in range(2)]
        t2 = [pg.alloc(f"t2_{i}", [128, NT], F32) for i in range(2)]
        rt1, rt2 = RL(2, "t1"), RL(2, "t2")
        for i in range(2):
            pg.memset("pool", vst[i], 1.0, [rv[i]])
        tiles = self.main_tiles() + [dict(b=0, i=0, ctx=True, col=2)]
        nbk = [0]

        def nb():
            nbk[0] += 1
            return nbk[0] % 7

        def load(idx):
            t = tiles[idx]
            s = idx % 2
            src, r = self.x_src(t)
            items = [(xt[s], src, [r], rxt[s])]
            if not t["ctx"]:
                t0 = t["i"] * NT
                items.append((rcs[s][:, 0, :], self.ropeC[:, t0:t0 + NT], [], [rrcs[s]]))
                items.append((rcs[s][:, 1, :], self.ropeS[:, t0:t0 + NT], [], []))
            pg.dma_batch("sp", items, key=("m1ld", s))

        def norm(idx):
            t = tiles[idx]
            col = t["col"]
            self.norm_mod(xt[idx % 2], rxt[idx % 2], NT, lambda kc: self.A1[:, l, kc, col:col + 1],
                          lambda kc: self.mod_col(l, 0, kc, col), hb[idx % 2], rhb[idx % 2], scratch)

        def compute(idx):
            t = tiles[idx]
            s = idx % 2
            col = t["col"]
            is_ctx = t["ctx"]
            h, rh = hb[s], rhb[s]

            def proj(cj, bk):
                for kc in range(KC):
                    pg.mm(self.banks[bk], W[:, kc, cj * 128:(cj + 1) * 128], h[:, kc, :], kc == 0, kc == KC - 1,
                          [rW[kc // 2], rh[kc]], [self.r_bank[bk]])

            def roped(cj, cjr, dst, rdst, k):
                bq = nb()
                proj(cj, bq)
                if is_ctx:
                    pg.copy("act", dst, self.banks[bq], [self.r_bank[bq]], [rdst])
                    return
                br = nb()
                proj(cjr, br)
                pg.tt("dve", t1[k % 2], self.banks[bq], rcs[s][:, 0, :], ALU.mult, [self.r_bank[bq], rrcs[s]], [rt1[k % 2]])
                pg.tt("dve", t2[k % 2], self.banks[br], rcs[s][:, 1, :], ALU.mult, [self.r_bank[br], rrcs[s]], [rt2[k % 2]])
                pg.tt("pool", dst, t1[k % 2], t2[k % 2], ALU.add, [rt1[k % 2], rt2[k % 2]], [rdst])

            full = (not is_ctx) or ctx_full
            if full:
                for c in range(4):
                    bk = nb()
                    proj(c, bk)
                    pg.copy("act", bgst[s][:, c, :], self.banks[bk], [self.r_bank[bk]], [rbg[s][c]])
                for c in range(4):
                    bc = nb()
                    proj(4 + c, bc)
                    bh = nb()
                    proj(8 + c, bh)
                    pg.copy("act", ctmp[c % 2], self.banks[bc], [self.r_bank[bc]], [rct[c % 2]])
                    pg.tt("dve", ust[s][:, c, :], ctmp[c % 2], self.banks[bh], ALU.mult, [rct[c % 2], self.r_bank[bh]],
                          [rus[s][c]])
                if idx + 1 < len(tiles):
                    norm(idx + 1)
                for c in range(4):
                    roped(12 + c, 16 + c, qst[s][:, c, :], rq[s][c], c)
            elif idx + 1 < len(tiles):
                norm(idx + 1)
            roped(20, 21, kkst[s], rkk[s][0], 0)
            bv = nb()
            for blk in range(4):
                for kc in range(KC):
                    pg.mm(self.banks[bv][:, blk * 128:(blk + 1) * 128], h[:, kc, blk * 128:(blk + 1) * 128],
                          W[:, kc, 22 * 128:23 * 128], kc == 0, kc == KC - 1, [rW[kc // 2], rh[kc]], [self.r_bank[bv]])
            vout = vst[s].rearrange("p b (g x) -> p b g x", g=2)[:, :, :, 0:64]
            vin = self.banks[bv].rearrange("p (b g x) -> p b g x", b=4, g=2)
            pg.copy("act", vout, vin, [self.r_bank[bv]], [rv[s]])

        def store(idx):
            t = tiles[idx]
            s = idx % 2
            is_ctx = t["ctx"]
            full = (not is_ctx) or ctx_full
            if is_ctx:
                dr = self.r_m1c
                dbg = self.bgc.rearrange("(c p) t -> p c t", p=128)
                du = self.uc.rearrange("(c p) t -> p c t", p=128)
                dq = self.qc.rearrange("(c p) t -> p c t", p=128)
                dkk = self.kkc
                dv = self.vc.rearrange("(blk p) f -> p blk f", p=128)
            else:
                b, i = t["b"], t["i"]
                sl = slice(i * NT, (i + 1) * NT)
                dr = self.r_m1[b][i]
                dbg = self.bgs[b].rearrange("(c p) t -> p c t", p=128)[:, :, sl]
                du = self.us[b].rearrange("(c p) t -> p c t", p=128)[:, :, sl]
                dq = self.qs[b].rearrange("(c p) t -> p c t", p=128)[:, :, sl]
                dkk = self.kks[b][:, sl]
                dv = self.vs[b].rearrange("(blk p) f -> p blk f", p=128)[:, 4 * i:4 * i + 4, :]
            items = []
            if full:
                items += [(dbg, bgst[s], rbg[s], [dr[0]]), (du, ust[s], rus[s], [dr[1]]), (dq, qst[s], rq[s], [dr[2]])]
            items += [(dkk, kkst[s], rkk[s], [dr[3]]), (dv, vst[s], [rv[s]], [dr[4]])]
            pg.dma_batch("sp", items, key=("m1st", s))

        load(0)
        norm(0)
        for idx in range(len(tiles)):
            if idx + 1 < len(tiles):
                load(idx + 1)
            compute(idx)
            store(idx)

    def phase_m2(self, l, with_ctx):
        pg = self.pg
        e = l // 2
        pg.phase_begin()
        Wc = pg.alloc("Wc", [128, 4, D], BF16)
        Wa = pg.alloc("Wa", [64, 8, D], BF16)
        rWc, rWa = Res("Wc"), Res("Wa")
        pg.dma("pool", Wc, self.w_mix[e][0:512, :].rearrange("(c p) n -> p c n", p=128), [], [rWc], key=("Wc",))
        pg.dma("pool", Wa, self.w_mix[e][512:1024, :].rearrange("(h p) n -> p h n", p=64), [], [rWa], key=("Wa",))
        self.masks_sb = pg.alloc("masks", [128, 2, 512], BF16)
        self.ident_sb = pg.alloc("ident", [128, 128], BF16)
        self.sel_sb = pg.alloc("sel", [1, 128], BF16)
        rmk = Res("maskconst")
        pg.dma_batch("sp", [(self.masks_sb, self.masks, [], [rmk]), (self.ident_sb, self.ident, [], []),
                            (self.sel_sb, self.selrow, [], [])], key=("m2const",))
        eskz = pg.alloc("eskz", [1, 2, NT], F32)
        esk = pg.alloc("esk", [1, 2, NT], BF16)
        resk = Res("esk")
        pg.memset("pool", eskz, 0.0, [resk])
        for g in range(2):
            for j in range(4):
                pg.ts("pool", esk[0:1, g, j * 128:(j + 1) * 128], eskz[0:1, g, j * 128:(j + 1) * 128],
                      self.esink[0:1, e, 4 * g + j:4 * g + j + 1], None, ALU.add, None, [resk, self.r_const], [resk])
        kctx = pg.alloc("kctx", [64, 2, 2 * CTX], BF16)
        vctx = pg.alloc("vctx", [128, 4, 256], BF16)
        rkctx, rvctx = Res("kctx"), Res("vctx")
        pg.dma_batch("sp", [
            (kctx, self.kkc.rearrange("(g p) t -> p g t", p=64), [self.r_m1c[3]], [rkctx]),
            (vctx, self.vc.rearrange("(blk p) f -> p blk f", p=128), [self.r_m1c[4]], [rvctx]),
        ], key=("ctxkv",))
        xt = [pg.alloc(f"xt{i}", [128, KC, NT], F32) for i in range(2)]
        bgt = [pg.alloc(f"bgt{i}", [128, 4, NT], F32) for i in range(2)]
        ut = [pg.alloc(f"ut{i}", [128, 4, NT + 2], F32) for i in range(2)]
        qt = [pg.alloc(f"qt{i}", [64, 8, NT], BF16) for i in range(2)]
        kkt = [pg.alloc(f"kkt{i}", [64, 2, NT + 256], BF16) for i in range(2)]
        vt = [pg.alloc(f"vt{i}", [128, 6, 256], BF16) for i in range(2)]
        rxt = [RL(KC, f"xt{i}_") for i in range(2)]
        rbgt, rut, rqt, rkkt, rvt = RL(2, "bgt"), RL(2, "ut"), RL(2, "qt"), RL(2, "kkt"), RL(2, "vt")
        pt = [pg.alloc(f"pt{i}", [128, 5, NT], BF16) for i in range(2)]
        rpt = [RL(5, f"pt{i}_") for i in range(2)]
        aout = [pg.alloc(f"aout{i}", [128, 4, NT], BF16) for i in range(2)]
        raout = [RL(4, f"aout{i}_") for i in range(2)]
        bout = [pg.alloc(f"bout{i}", [64, 8, NT], BF16) for i in range(2)]
        rbout = [RL(2, f"bout{i}_") for i in range(2)]
        ytmp = pg.alloc("ytmp", [128, 4, NT], F32)
        rytmp = RL(4, "ytmp")
        dsum = [pg.alloc(f"dsum{i}", [64, NT], F32) for i in range(2)]
        rdsum = RL(2, "dsum")
        tiles = [dict(b=b, i=i, ctx=False, col=b, nt=NT) for b in range(2) for i in range(NTILE)]
        if with_ctx:
            tiles += [dict(b=b, i=0, ctx=True, col=2, nt=CTX) for b in range(2)]
        cnt = dict(sb=0, ob=0, slot=0)

        def load(idx):
            t = tiles[idx]
            s = idx % 2
            b, i, nt = t["b"], t["i"], t["nt"]
            if t["ctx"]:
                sl = slice(b * CTX, (b + 1) * CTX)
                xsrc = self.xc_cur.rearrange("(kc p) t -> p kc t", p=128)[:, :, sl]
                rx = self.r_xc
                dr = self.r_m1c
                pg.memset("pool", ut[s], 0.0, [rut[s]])
                items = [
                    (xt[s][:, :, :nt], xsrc, [rx], rxt[s]),
                    (bgt[s][:, :, :nt], self.bgc.rearrange("(c p) t -> p c t", p=128)[:, :, sl], [dr[0]], [rbgt[s]]),
                    (ut[s][:, :, 1:nt + 1], self.uc.rearrange("(c p) t -> p c t", p=128)[:, :, sl], [dr[1]], [rut[s]]),
                    (qt[s][:, :, :nt], self.qc.rearrange("(h p) t -> p h t", p=64)[:, :, sl], [dr[2]], [rqt[s]]),
                ]
            else:
                t0 = i * NT
                xsrc, rx = self.x_src(t)
                nbrs = [self.r_m1[b][k] for k in (i - 1, i, i + 1) if 0 <= k < NTILE]
                if i == 0 or i == NTILE - 1:
                    pg.memset("pool", ut[s], 0.0, [rut[s]])
                ulo, uhi = max(t0 - 1, 0), min(t0 + NT + 1, SEQ)
                klo, khi = max(t0 - 128, 0), min(t0 + NT + 128, SEQ)
                blo, bhi = klo // 128, khi // 128
                items = [
                    (xt[s], xsrc, [rx], rxt[s]),
                    (bgt[s], self.bgs[b].rearrange("(c p) t -> p c t", p=128)[:, :, t0:t0 + NT], [self.r_m1[b][i][0]], [rbgt[s]]),
                    (ut[s][:, :, ulo - (t0 - 1):uhi - (t0 - 1)], self.us[b].rearrange("(c p) t -> p c t", p=128)[:, :, ulo:uhi],
                     [r[1] for r in nbrs], [rut[s]]),
                    (qt[s], self.qs[b].rearrange("(h p) t -> p h t", p=64)[:, :, t0:t0 + NT], [self.r_m1[b][i][2]], [rqt[s]]),
                    (kkt[s][:, :, klo - (t0 - 128):khi - (t0 - 128)], self.kks[b].rearrange("(g p) t -> p g t", p=64)[:, :, klo:khi],
                     [r[3] for r in nbrs], [rkkt[s]]),
                    (vt[s][:, blo - (4 * i - 1):bhi - (4 * i - 1), :], self.vs[b].rearrange("(blk p) f -> p blk f", p=128)[:, blo:bhi, :],
                     [r[4] for r in nbrs], [rvt[s]]),
                ]
            pg.dma_batch("sp", items, key=("m2ld", s))

        def conv(idx):
            t = tiles[idx]
            s = idx % 2
            nt = t["nt"]
            cw = lambda tap, c: self.conv_sb[:, e, tap, c:c + 1]
            for c in range(4):
                pg.act(ytmp[:, c, :nt], ut[s][:, c, 1:nt + 1], AF.Copy, [rut[s], self.r_const], [rytmp[c]], scale=cw(1, c))
            for c in range(4):
                pg.stt("dve", ytmp[:, c, :nt], ut[s][:, c, 0:nt], cw(0, c), ytmp[:, c, :nt], ALU.mult, ALU.add,
                       [rut[s], self.r_const, rytmp[c]], [rytmp[c]])
            for c in range(4):
                pg.stt("dve", ytmp[:, c, :nt], ut[s][:, c, 2:nt + 2], cw(2, c), ytmp[:, c, :nt], ALU.mult, ALU.add,
                       [rut[s], self.r_const, rytmp[c]], [rytmp[c]])
            for c in range(4):
                pg.tt("pool", aout[s][:, c, :nt], bgt[s][:, c, :nt], ytmp[:, c, :nt], ALU.mult, [rbgt[s], rytmp[c]],
                      [raout[s][c]])

        def make(n):
            idx, qb, g = groups[n]
            t = tiles[idx]
            s = idx % 2
            b, i, is_ctx = t["b"], t["i"], t["ctx"]
            ents = []
            if not is_ctx:
                G = 4 * i + qb
                if G > 0:
                    ents.append((kkt[s][:, g, qb * 128:(qb + 1) * 128],
                                 vt[s][:, qb, g * 128:(g + 1) * 128], 0, rkkt[s], rvt[s]))
                ents.append((kkt[s][:, g, (qb + 1) * 128:(qb + 2) * 128],
                             vt[s][:, qb + 1, g * 128:(g + 1) * 128], None, rkkt[s], rvt[s]))
                if G < SEQ // 128 - 1:
                    ents.append((kkt[s][:, g, (qb + 2) * 128:(qb + 3) * 128],
                                 vt[s][:, qb + 2, g * 128:(g + 1) * 128], 1, rkkt[s], rvt[s]))
            for j2 in range(2):
                ents.append((kctx[:, g, b * CTX + j2 * 128:b * CTX + (j2 + 1) * 128],
                             vctx[:, 2 * b + j2, g * 128:(g + 1) * 128], None, rkctx, rvctx))
            return dict(idx=idx, qb=qb, g=g, s=s, ents=ents, slot=n % 2, n=n)

        SB = (0, 1, 2)

        def scores(grp):
            s, qb, g, slot = grp["s"], grp["qb"], grp["g"], grp["slot"]
            for kbi, (kap, vap, mk, rk, rvv) in enumerate(grp["ents"]):
                bk = SB[cnt["sb"] % len(SB)]
                cnt["sb"] += 1
                pg.mm(self.banks[bk].rearrange("p (j q) -> p j q", j=4), kap,
                      qt[s][:, 4 * g:4 * g + 4, qb * 128:(qb + 1) * 128], True, mk is None,
                      [rk, rqt[s]], [self.r_bank[bk]])
                if mk is not None:
                    pg.mm(self.banks[bk], self.ident_sb, self.masks_sb[:, mk, :], False, True,
                          [rmk], [self.r_bank[bk]])
                pg.act(pt[slot][:, kbi, :], self.banks[bk], AF.Exp, [self.r_bank[bk]], [rpt[slot][kbi]], scale=0.125)

        def pv_epi(grp):
            s, qb, g, slot, n = grp["s"], grp["qb"], grp["g"], grp["slot"], grp["n"]
            ents = grp["ents"]
            ob = 3 + n % 2
            ds = n % 2
            for kbi, (kap, vap, mk, rk, rvv) in enumerate(ents):
                pg.mm(self.banks[ob], vap, pt[slot][:, kbi, :], kbi == 0, False,
                      [rvv, rpt[slot][kbi]], [self.r_bank[ob]])
            pg.mm(self.banks[ob], self.sel_sb[0:1, :], esk[0:1, g, :], False, True, [rmk, resk], [self.r_bank[ob]])
            pg.op("dve", lambda v, o=dsum[ds], i_=self.banks[ob][64:128, :]: v.reciprocal(o, i_), [self.r_bank[ob]], [rdsum[ds]])
            pg.tt("dve", bout[s][:, 4 * g:4 * g + 4, qb * 128:(qb + 1) * 128],
                  self.banks[ob][0:64, :].rearrange("p (j q) -> p j q", j=4),
                  dsum[ds].rearrange("p (j q) -> p j q", j=4), ALU.mult,
                  [self.r_bank[ob], rdsum[ds]], [rbout[s][g]])

        def mix(idx):
            t = tiles[idx]
            s = idx % 2
            nt, col = t["nt"], t["col"]
            X, rX = xt[s], rxt[s]
            for oc in range(KC):
                bk = 5 + oc % 2
                for c in range(4):
                    pg.mm(self.banks[bk][:, :nt], Wc[:, c, oc * 128:(oc + 1) * 128], aout[s][:, c, :nt], c == 0, False,
                          [rWc, raout[s][c]], [self.r_bank[bk]])
                for hh in range(8):
                    pg.mm(self.banks[bk][:, :nt], Wa[:, hh, oc * 128:(oc + 1) * 128], bout[s][:, hh, :nt], False, hh == 7,
                          [rWa, rbout[s][hh // 4]], [self.r_bank[bk]])
                pg.stt("dve", X[:, oc, :nt], self.banks[bk][:, :nt], self.mod_col(l, 2, oc, col), X[:, oc, :nt],
                       ALU.mult, ALU.add, [self.r_bank[bk], rX[oc], self.r_mod], [rX[oc]])

        def store(idx):
            t = tiles[idx]
            s = idx % 2
            if t["ctx"]:
                b = t["b"]
                dst = self.xcs.rearrange("(kc p) t -> p kc t", p=128)[:, :, b * CTX:(b + 1) * CTX]
                pg.dma("sp", dst, xt[s][:, :, :CTX], rxt[s], [self.r_xc], key=("xst", s))
            else:
                dst, r = self.x_dst(t)
                pg.dma("sp", dst, xt[s], rxt[s], [r], key=("xst", s))

        groups = [(idx, qb, g) for idx, t in enumerate(tiles) for qb in range(t["nt"] // 128) for g in range(2)]
        load(0)
        if len(tiles) > 1:
            load(1)
        conv(0)
        cur = make(0)
        scores(cur)
        for n in range(len(groups)):
            idx = groups[n][0]
            nxt = None
            if n + 1 < len(groups):
                if groups[n + 1][0] != idx:
                    conv(groups[n + 1][0])
                nxt = make(n + 1)
                scores(nxt)
            pv_epi(cur)
            if n + 1 == len(groups) or groups[n + 1][0] != idx:
                mix(idx)
                store(idx)
                if idx + 2 < len(tiles):
                    load(idx + 2)
            cur = nxt
        self.x_cur = self.xs
        if with_ctx:
            self.xc_cur = self.xcs

    def phase_f1(self, l, with_ctx):
        pg = self.pg
        pg.phase_begin()
        cs = pg.alloc("cs", [128, 2, 512], BF16)
        rcs = Res("cs")
        pg.dma("sp", cs, self.cs256.rearrange("(kc p) f -> p kc f", p=128), [], [rcs], key=("cs",))
        xt = [pg.alloc(f"xt{i}", [128, KC, NT], F32) for i in range(2)]
        rxt = [RL(KC, f"xt{i}_") for i in range(2)]
        hb = [pg.alloc(f"h{i}", [128, KC, NT], BF16) for i in range(2)]
        rhb = [RL(KC, f"h{i}_") for i in range(2)]
        scratch = self.norm_scratch(ssb=7)
        zst = [pg.alloc(f"zst{i}", [128, 4, 2048], BF16) for i in range(2)]
        rzst = [RL(8, f"zst{i}_") for i in range(2)]
        tiles = self.main_tiles()
        if with_ctx:
            tiles.append(dict(b=0, i=0, ctx=True, col=2))
        nbk = [0]

        def load(idx):
            src, r = self.x_src(tiles[idx])
            pg.dma("sp", xt[idx % 2], src, [r], rxt[idx % 2], key=("xt", idx % 2))

        def norm(idx):
            col = tiles[idx]["col"]
            self.norm_mod(xt[idx % 2], rxt[idx % 2], NT, lambda kc: self.A1[:, l, kc, col:col + 1],
                          lambda kc: self.mod_col(l, 0, kc, col), hb[idx % 2], rhb[idx % 2], scratch)

        def compute(idx):
            t = tiles[idx]
            s = idx % 2
            h, rh = hb[s], rhb[s]
            for blk in range(4):
                if blk == 1 and idx + 1 < len(tiles):
                    norm(idx + 1)
                for g in range(4):
                    bk = nbk[0] % 7
                    nbk[0] += 1
                    for k2 in range(2):
                        pg.mm(self.banks[bk], h[:, 2 * g + k2, blk * 128:(blk + 1) * 128], cs[:, k2, :], k2 == 0, k2 == 1,
                              [rh[2 * g + k2], rcs], [self.r_bank[bk]])
                    zv = zst[s][:, blk, g * 512:(g + 1) * 512].rearrange("p (hf c k) -> p hf c k", hf=2, c=2)
                    pv = self.banks[bk].rearrange("p (c hf k) -> p hf c k", c=2, hf=2)
                    pg.copy("act" if g % 2 == 0 else "dve", zv, pv, [self.r_bank[bk]], [rzst[s][2 * blk + g % 2]])

        def store(idx):
            t = tiles[idx]
            s = idx % 2
            if t["ctx"]:
                dst, r = self.zc.rearrange("(blk p) f -> p blk f", p=128), self.r_zc
            else:
                b, i = t["b"], t["i"]
                dst = self.zs[b].rearrange("(blk p) f -> p blk f", p=128)[:, 4 * i:4 * i + 4, :]
                r = self.r_z[b][i]
            pg.dma("sp", dst, zst[s], rzst[s], [r], key=("zst", s))

        load(0)
        norm(0)
        for idx in range(len(tiles)):
            if idx + 1 < len(tiles):
                load(idx + 1)
            compute(idx)
            store(idx)

    def phase_f2(self, l, with_ctx):
        pg = self.pg
        pg.phase_begin()
        NF = 16
        Zf = pg.alloc("Zf", [128, NF, 2048], BF16)
        rZf = [RL(4, f"Zf{i}_") for i in range(NF)]
        Za = [pg.alloc(f"Za{i}", [128, 2, 2048], BF16) for i in range(2)]
        Zb = [pg.alloc(f"Zb{i}", [128, 2, 2048], BF16) for i in range(2)]
        rZa, rZb = RL(2, "Za"), RL(2, "Zb")
        Z0 = pg.alloc("Z0", [1, 2048], BF16)
        rZ0 = Res("Z0")
        Tb = [pg.alloc(f"Tb{i}", [128, 2, NF, NT], BF16) for i in range(2)]
        rTb = RL(2, "Tb")
        fst = [pg.alloc(f"fst{i}", [128, NT], BF16) for i in range(4)]
        rfst = RL(4, "fst")
        ijn = pg.alloc("ijn", [128, 3, 128], BF16)
        c0 = pg.alloc("c0", [1, NT], BF16)
        cs = pg.alloc("cs", [128, 2, 512], BF16)
        rcs = Res("cs")
        rij = Res("ij")
        pg.dma_batch("sp", [(cs, self.csp256.rearrange("(kc p) f -> p kc f", p=128), [], [rcs]),
                            (ijn, self.ijn, [], [rij]), (c0, self.c0row, [], [])], key=("cs",))
        cnsrc = self.cnf.rearrange("(nc p) k -> p nc k", p=128)
        snsrc = self.snf.rearrange("(nc p) k -> p nc k", p=128)
        nev = 0
        npiece = 0
        nfold = 0
        for b in range(2):
            zrows = self.zs[b]
            pg.dma("sp", Z0, zrows[0:1, :], self.r_z[b], [rZ0], key=("Z0",))
            for pr in range(NF // 2):
                sl = pr % 2
                a_src = zrows[1 + 256 * pr:1 + 256 * pr + 256, :].rearrange("(c p) f -> p c f", p=128)
                items = [(Za[sl], a_src, self.r_z[b], [rZa[sl]])]
                for q in range(2):
                    nc_ = 2 * pr + q
                    lo = SEQ - 128 - 128 * nc_
                    items.append((Zb[sl][:, q, :], zrows[lo:lo + 128, :], self.r_z[b], [rZb[sl]] if q == 0 else []))
                pg.dma_batch("sp", items, key=("Zab", sl))
                for q in range(2):
                    nc_ = 2 * pr + q
                    for g4 in range(4):
                        bk = nfold % 7
                        nfold += 1
                        cols = slice(g4 * 512, (g4 + 1) * 512)
                        pg.mm(self.banks[bk], ijn[:, 0, :], Za[sl][:, q, cols], True, False, [rij, rZa[sl]], [self.r_bank[bk]])
                        for blk in range(4):
                            which = 1 if blk % 2 == 0 else 2
                            pg.mm(self.banks[bk][:, blk * 128:(blk + 1) * 128], ijn[:, which, :],
                                  Zb[sl][:, q, g4 * 512 + blk * 128:g4 * 512 + (blk + 1) * 128], False, blk == 3,
                                  [rij, rZb[sl]], [self.r_bank[bk]])
                        pg.copy("act" if nfold % 2 == 0 else "dve", Zf[:, nc_, cols], self.banks[bk], [self.r_bank[bk]], [rZf[nc_][g4]])
            for j in range(NTILE):
                tb = npiece % 2
                npiece += 1
                pg.dma_batch("sp", [
                    (Tb[tb][:, 0, 0:8, :], cnsrc[:, 0:8, j * NT:(j + 1) * NT], [], [rTb[tb]]),
                    (Tb[tb][:, 0, 8:16, :], cnsrc[:, 8:16, j * NT:(j + 1) * NT], [], []),
                    (Tb[tb][:, 1, 0:8, :], snsrc[:, 0:8, j * NT:(j + 1) * NT], [], []),
                    (Tb[tb][:, 1, 8:16, :], snsrc[:, 8:16, j * NT:(j + 1) * NT], [], []),
                ], key=("Tb", tb))
                for fg in range(2):
                    for c2 in range(2):
                        for m in range(4):
                            bk = 4 * fg + m
                            base = fg * 1024 + m * 256 + c2 * 128
                            for nch in range(NF):
                                pg.mm(self.banks[bk], Zf[:, nch, base:base + 128], Tb[tb][:, c2, nch, :],
                                      c2 == 0 and nch == 0, False, [rZf[nch][base // 512], rTb[tb]], [self.r_bank[bk]])
                    for m in range(4):
                        bk = 4 * fg + m
                        base = fg * 1024 + m * 256
                        pg.mm(self.banks[bk], Z0[0:1, base:base + 128], c0[0:1, :], False, True, [rZ0, rcs], [self.r_bank[bk]])
                    for m in range(4):
                        bk = 4 * fg + m
                        k = nev % 4
                        nev += 1
                        fcn = 4 * fg + m
                        pg.copy("act" if m % 2 == 0 else "dve", fst[k], self.banks[bk], [self.r_bank[bk]], [rfst[k]])
                        pg.dma("sp", self.fs[b][fcn * 128:(fcn + 1) * 128, j * NT:(j + 1) * NT], fst[k], [rfst[k]],
                               [self.r_f[b][j][fcn]], key=("fst", k))
        if with_ctx:
            Zc = pg.alloc("Zc", [128, 4, 2048], BF16)
            rZc = Res("Zc")
            pg.dma("sp", Zc, self.zc.rearrange("(blk p) f -> p blk f", p=128), [self.r_zc], [rZc], key=("Zc",))
            for b in range(2):
                for fcn in range(8):
                    bk = fcn
                    fg, m = fcn // 4, fcn % 4
                    for c2 in range(2):
                        for nch in range(2):
                            pg.mm(self.banks[bk][:, 0:CTX], Zc[:, 2 * b + nch, fg * 1024 + m * 256 + c2 * 128:fg * 1024 + m * 256 + (c2 + 1) * 128],
                                  cs[:, nch, c2 * 256:(c2 + 1) * 256], c2 == 0 and nch == 0, c2 == 1 and nch == 1,
                                  [rZc, rcs], [self.r_bank[bk]])
                    k = nev % 4
                    nev += 1
                    pg.copy("act" if fcn % 2 == 0 else "dve", fst[k][:, 0:CTX], self.banks[bk][:, 0:CTX], [self.r_bank[bk]], [rfst[k]])
                    pg.dma("sp", self.fc[fcn * 128:(fcn + 1) * 128, b * CTX:(b + 1) * CTX], fst[k][:, 0:CTX], [rfst[k]],
                           [self.r_fc[b * 8 + fcn]], key=("fst", k))

    def phase_f3(self, l, with_ctx):
        pg = self.pg
        o = l // 2
        pg.phase_begin()
        Wf = pg.alloc("Wf", [128, KC, D], BF16)
        rWf = RL(2, "Wf")
        wsrc = self.w_four[o].rearrange("(kc p) n -> p kc n", p=128)
        for hlf in range(2):
            pg.dma("pool", Wf[:, 4 * hlf:4 * hlf + 4, :], wsrc[:, 4 * hlf:4 * hlf + 4, :], [], [rWf[hlf]], key=("Wf", hlf))
        xt = [pg.alloc(f"xt{i}", [128, KC, NT], F32) for i in range(2)]
        rxt = [RL(KC, f"xt{i}_") for i in range(2)]
        ft = [pg.alloc(f"ft{i}", [128, KC, NT], BF16) for i in range(2)]
        rft = RL(2, "ft")
        tiles = self.main_tiles()
        if with_ctx:
            tiles.append(dict(b=0, i=0, ctx=True, col=2))

        def load(idx):
            t = tiles[idx]
            s = idx % 2
            src, r = self.x_src(t)
            if t["ctx"]:
                fsrc, fr = self.fc.rearrange("(kc p) t -> p kc t", p=128), self.r_fc
            else:
                b, i = t["b"], t["i"]
                fsrc = self.fs[b].rearrange("(kc p) t -> p kc t", p=128)[:, :, i * NT:(i + 1) * NT]
                fr = self.r_f[b][i]
            pg.dma_batch("sp", [(xt[s], src, [r], rxt[s]), (ft[s], fsrc, fr, [rft[s]])], key=("f3ld", s))

        def compute(idx):
            t = tiles[idx]
            s = idx % 2
            col = t["col"]
            for oc in range(KC):
                bk = oc % 4
                for kc in range(KC):
                    pg.mm(self.banks[bk], Wf[:, kc, oc * 128:(oc + 1) * 128], ft[s][:, kc, :], kc == 0, kc == KC - 1,
                          [rWf[kc // 4], rft[s]], [self.r_bank[bk]])
                pg.stt("dve", xt[s][:, oc, :], self.banks[bk], self.mod_col(l, 2, oc, col), xt[s][:, oc, :], ALU.mult, ALU.add,
                       [self.r_bank[bk], rxt[s][oc], self.r_mod], [rxt[s][oc]])

        def store(idx):
            dst, r = self.x_dst(tiles[idx])
            pg.dma("sp", dst, xt[idx % 2], rxt[idx % 2], [r], key=("xst", idx % 2))

        load(0)
        for idx in range(len(tiles)):
            if idx + 1 < len(tiles):
                load(idx + 1)
            compute(idx)
            store(idx)
        self.x_cur = self.xs
        if with_ctx:
            self.xc_cur = self.xcs

    def build(self):
        self.load_consts()
        self.phase_adaln()
        for l in self.layers:
            ctx_after = any(j % 2 == 0 for j in range(l + 1, DEPTH))
            last = (l == self.layers[-1])
            if self.do_mixer:
                if l % 2 == 0:
                    self.phase_m1(l, ctx_full=ctx_after)
                    if "nom2" not in self.dbg:
                        self.phase_m2(l, with_ctx=ctx_after)
                else:
                    self.phase_f1(l, with_ctx=ctx_after)
                    if "nof2" not in self.dbg:
                        self.phase_f2(l, with_ctx=ctx_after)
                    if "nof3" not in self.dbg:
                        self.phase_f3(l, with_ctx=ctx_after)
            if self.do_ffn:
                self.phase_ffn(l, with_ctx=ctx_after, last=last)
        self.pg.barrier()
        self.pg.emit()
        return self.nc


def _bf16(a):
    return np.ascontiguousarray(a.astype(ml_dtypes.bfloat16))


_CONST_CACHE = {}


def _constants():
    if _CONST_CACHE:
        return _CONST_CACHE
    n = np.arange(256, dtype=np.int64)
    ang = 2.0 * np.pi * ((n[:, None] * n[None, :]) % 256) / 256.0
    cs256 = np.concatenate([np.cos(ang), -np.sin(ang)], axis=1) / 16.0
    csp256 = np.concatenate([np.cos(ang), np.sin(ang)], axis=1) / 16.0
    m = np.arange(SEQ, dtype=np.int64)
    nn = np.arange(1, SEQ // 2 + 1, dtype=np.int64)
    angN = 2.0 * np.pi * ((nn[:, None] * m[None, :]) % SEQ) / SEQ
    cn = np.cos(angN) / 64.0
    sn = np.sin(angN) / 64.0
    cn[-1] *= 0.5
    sn[-1] = 0.0
    eye = np.eye(128, dtype=np.float32)
    ijn = np.stack([eye, eye[::-1], -eye[::-1]], axis=1)
    c0row = np.full((1, NT), 1.0 / 64.0, np.float32)
    t = np.arange(SEQ)
    r = (t // 64).astype(np.float32)
    col = (t % 64).astype(np.float32)
    inv = (np.float32(10000.0) ** (-np.arange(16, dtype=np.float32) / np.float32(16))).astype(np.float32)
    ropeC = np.zeros((128, SEQ), np.float32)
    ropeS = np.zeros((128, SEQ), np.float32)
    for p in range(128):
        d = p % 64
        pos = r if d < 32 else col
        a = (pos * inv[d % 16]).astype(np.float32)
        sign = -1.0 if (d % 32) < 16 else 1.0
        ropeC[p] = np.cos(a)
        ropeS[p] = sign * np.sin(a)
    kj = np.arange(128)[:, None]
    qi = np.arange(128)[None, :]
    valid = np.stack([(qi <= kj), (kj <= qi)], axis=1)
    masks = np.where(valid, 0.0, -30000.0).astype(np.float32)
    masks = np.ascontiguousarray(np.tile(masks, (1, 1, 4)))
    ident = np.eye(128, dtype=np.float32)
    selrow = np.concatenate([np.zeros((1, 64), np.float32), np.ones((1, 64), np.float32)], axis=1)
    _CONST_CACHE.update(
        cs256=_bf16(cs256.astype(np.float32)), csp256=_bf16(csp256.astype(np.float32)), cnf=_bf16(cn.astype(np.float32)), snf=_bf16(sn.astype(np.float32)),
        ijn=_bf16(ijn), c0row=_bf16(c0row),
        ropeC=ropeC, ropeS=ropeS, masks=_bf16(masks), ident=_bf16(ident), selrow=_bf16(selrow))
    return _CONST_CACHE


def _win_ext(w_in):
    Q0, K0, V0 = 1536, 2048, 2176
    partner = np.array([d + 16 if (d % 32) < 16 else d - 16 for d in range(64)])
    cols = list(range(0, 1536))
    cols += list(range(Q0, Q0 + 512))
    for hh in range(8):
        cols += list(Q0 + hh * 64 + partner)
    cols += list(range(K0, K0 + 128))
    for g in range(2):
        cols += list(K0 + g * 64 + partner)
    cols += list(range(V0, V0 + 128))
    cols = np.array(cols)
    assert cols.shape[0] == WIN_COLS
    return np.ascontiguousarray(w_in[:, :, cols])


def _shared_inputs(inp):
    f = np.float32
    sh = {}
    sh["w_ada"] = np.ascontiguousarray(inp["w_ada"], dtype=f)
    sh["b_adaT"] = np.ascontiguousarray(inp["b_ada"].reshape(DEPTH, 48, 128).transpose(2, 0, 1), dtype=f)
    sh["g1T"] = np.ascontiguousarray(inp["norm1_g"].reshape(DEPTH, KC, 128).transpose(2, 0, 1), dtype=f)
    sh["g2T"] = np.ascontiguousarray(inp["norm2_g"].reshape(DEPTH, KC, 128).transpose(2, 0, 1), dtype=f)
    sh["gfT"] = np.ascontiguousarray(inp["final_g"].reshape(KC, 128).T, dtype=f)
    sh["w_in_ext"] = _win_ext(np.asarray(inp["w_in"], dtype=f))
    sh["convT"] = np.ascontiguousarray(inp["conv_w"].reshape(2, 3, 4, 128).transpose(3, 0, 1, 2), dtype=f)
    sh["sinkb"] = np.ascontiguousarray(np.broadcast_to(inp["sink"][None], (128, 2, 8)), dtype=f)
    sh["w_mix_out"] = np.ascontiguousarray(inp["w_mix_out"], dtype=f)
    sh["w_fourier_out"] = np.ascontiguousarray(inp["w_fourier_out"], dtype=f)
    sh["w_ffn_gate"] = np.ascontiguousarray(inp["w_ffn_gate"], dtype=f)
    sh["w_ffn_up"] = np.ascontiguousarray(inp["w_ffn_up"], dtype=f)
    sh["w_ffn_down"] = np.ascontiguousarray(inp["w_ffn_down"], dtype=f)
    sh.update(_constants())
    return sh


def _core_inputs(inp, i):
    f = np.float32
    b0 = 2 * i
    m = {}
    m["xT"] = np.ascontiguousarray(np.asarray(inp["x"][b0:b0 + 2], dtype=f).transpose(0, 2, 1))
    m["ctxT"] = np.ascontiguousarray(np.asarray(inp["ctx"][b0:b0 + 2], dtype=f).transpose(2, 0, 1).reshape(D, 2 * CTX))
    cc = np.stack([inp["c"][b0], inp["c"][b0 + 1], inp["c_ctx"]], axis=0).astype(f)
    m["cT"] = np.ascontiguousarray(cc.reshape(3, KC, 128).transpose(2, 1, 0))
    return m


_NC_CACHE = {}


def _get_nc(**kw):
    key = tuple(sorted(kw.items()))
    if key not in _NC_CACHE:
        _NC_CACHE[key] = Builder(**kw).build()
    return _NC_CACHE[key]


def run_device(inputs, n_cores=N_CORES, **kw):
    nc = _get_nc(**kw)
    shared = _shared_inputs(inputs)
    in_maps = []
    for i in range(n_cores):
        m = dict(shared)
        m.update(_core_inputs(inputs, i))
        in_maps.append(m)
    res = run_bass_kernel_spmd(nc, in_maps, core_ids=list(range(n_cores)))
    if kw.get("dbg"):
        return [{k: np.asarray(v) for k, v in r.items()} for r in res.results]
    outs = [np.asarray(r["yT"]) for r in res.results]
    return outs


def kernel(**inputs):
    outs = run_device(inputs)
    y = np.concatenate([o.transpose(0, 2, 1) for o in outs], axis=0)
    return np.ascontiguousarray(y.astype(np.float32))
```

```python
import math
from contextlib import ExitStack

import numpy as np
import ml_dtypes

import concourse.bass as bass
import concourse.mybir as mybir
from concourse.bass_utils import run_bass_kernel_spmd

F32 = mybir.dt.float32
BF16 = mybir.dt.bfloat16
ALU = mybir.AluOpType
AF = mybir.ActivationFunctionType

N_CORES = 8
D = 1024
KC = 8
SEQ = 4096
NT = 512
NTILE = SEQ // NT
CTX = 256
DFF = 2816
FC = DFF // 128
DEPTH = 4
EPS = 1e-6
WIN_COLS = 23 * 128
SAME_ENGINE_SYNC = True

ENGS = ("pe", "act", "dve", "pool", "sp")
SB_BASE = 16512
SB_END = 229376


def _dtsize(dt):
    return 2 if dt == BF16 else 4


class Res:
    __slots__ = ("name", "w", "r")

    def __init__(self, name=""):
        self.name = name
        self.w = None
        self.r = {}


def RL(n, name=""):
    return [Res(f"{name}{i}") for i in range(n)]


class Op:
    __slots__ = ("fn", "deps", "dma", "sig", "signo")

    def __init__(self, fn, deps, dma):
        self.fn = fn
        self.deps = deps
        self.dma = dma
        self.sig = False
        self.signo = 0


class Prog:
    def __init__(self, nc):
        self.nc = nc
        self.ops = {e: [] for e in ENGS}
        self.dma_cnt = {}
        self.sb_ptr = SB_BASE
        self.phase_base = None
        self.uid = 0

    def alloc(self, name, shape, dtype):
        nbytes = int(np.prod(shape[1:])) * _dtsize(dtype)
        off = (self.sb_ptr + 31) // 32 * 32
        self.sb_ptr = off + nbytes
        assert self.sb_ptr <= SB_END, f"SBUF overflow allocating {name}: {self.sb_ptr}"
        self.uid += 1
        h = self.nc.alloc_sbuf_tensor_at(f"{name}_{self.uid}", list(shape), dtype, offset=off)
        return h.ap()

    def end_persistent(self):
        self.phase_base = self.sb_ptr

    def phase_begin(self):
        self.barrier()
        self.sb_ptr = self.phase_base

    def op(self, eng, fn, reads=(), writes=(), dma=None):
        deps = set()
        for r in reads:
            if r.w is not None:
                deps.add(r.w)
        for w in writes:
            if w.w is not None:
                deps.add(w.w)
            deps.update(w.r.values())
        rec = Op(fn, deps, dma)
        idx = len(self.ops[eng])
        self.ops[eng].append(rec)
        if dma is not None:
            self.dma_cnt[dma] = self.dma_cnt.get(dma, 0) + 16
            ev = ("D", dma, self.dma_cnt[dma])
            rkey = ("D", dma)
        else:
            ev = ("E", eng, idx)
            rkey = ("E", eng)
        for r in reads:
            r.r[rkey] = ev
        for w in writes:
            w.w = ev
            w.r = {}
        return rec

    def barrier(self):
        deps = set()
        for e in ENGS:
            for i in range(len(self.ops[e]) - 1, -1, -1):
                if self.ops[e][i].fn is not None and self.ops[e][i].dma is None:
                    deps.add(("E", e, i))
                    break
        for k, c in self.dma_cnt.items():
            deps.add(("D", k, c))
        for e in ENGS:
            self.ops[e].append(Op(None, set(deps), None))

    def mm(self, out, lhsT, rhs, start, stop, reads, writes):
        return self.op("pe", lambda t: t.matmul(out, lhsT, rhs, start=start, stop=stop), reads, writes)

    def dma(self, eng, out, in_, reads, writes, key):
        return self.op(eng, lambda q: q.dma_start(out=out, in_=in_), reads, writes, dma=key)

    def dma_batch(self, eng, items, key):
        for (out, in_, reads, writes) in items:
            self.dma(eng, out, in_, reads, writes, key)
        ev = ("D", key, self.dma_cnt[key])
        for (_, _, reads, writes) in items:
            for r in reads:
                r.r[("D", key)] = ev
            for w in writes:
                w.w = ev

    def act(self, out, in_, func, reads, writes, bias=None, scale=None, eng="act"):
        kw = {}
        if bias is not None:
            kw["bias"] = bias
        if scale is not None:
            kw["scale"] = scale
        return self.op(eng, lambda a: a.activation(out, in_, func, **kw), reads, writes)

    def tt(self, eng, out, in0, in1, op, reads, writes):
        return self.op(eng, lambda v: v.tensor_tensor(out, in0, in1, op), reads, writes)

    def ts(self, eng, out, in0, s1, s2, op0, op1, reads, writes):
        if s2 is None:
            return self.op(eng, lambda v: v.tensor_scalar(out, in0, s1, None, op0), reads, writes)
        return self.op(eng, lambda v: v.tensor_scalar(out, in0, s1, s2, op0, op1), reads, writes)

    def stt(self, eng, out, in0, scalar, in1, op0, op1, reads, writes):
        return self.op(eng, lambda v: v.scalar_tensor_tensor(out, in0, scalar, in1, op0, op1), reads, writes)

    def copy(self, eng, out, in_, reads, writes):
        if eng == "act":
            return self.op(eng, lambda a: a.copy(out, in_), reads, writes)
        return self.op(eng, lambda v: v.tensor_copy(out, in_), reads, writes)

    def memset(self, eng, ap, val, writes):
        return self.op(eng, lambda v: v.memset(ap, val), (), writes)

    def emit(self):
        nc = self.nc
        for e in ENGS:
            for op in self.ops[e]:
                for d in op.deps:
                    if d[0] == "E":
                        if d[1] == e and (e == "pe" or e == "sp" or not SAME_ENGINE_SYNC):
                            continue
                        self.ops[d[1]][d[2]].sig = True
        for e in ENGS:
            n = 0
            for op in self.ops[e]:
                if op.sig:
                    if op.fn is None:
                        op.sig = False
                        continue
                    n += 1
                    op.signo = n
        with ExitStack() as st:
            esem = {e: st.enter_context(nc.semaphore(f"s_{e}")) for e in ENGS}
            dsem = {k: st.enter_context(nc.semaphore(f"d_{i}")) for i, k in enumerate(self.dma_cnt)}
            block = st.enter_context(nc.Block())

            def run(e, eng):
                known = {}
                for op in self.ops[e]:
                    waits = {}
                    for d in op.deps:
                        if d[0] == "E":
                            if d[1] == e and (e == "pe" or e == "sp" or not SAME_ENGINE_SYNC):
                                continue
                            key = ("E", d[1])
                            val = self.ops[d[1]][d[2]].signo
                            sem = esem[d[1]]
                        else:
                            key = ("D", d[1])
                            val = d[2]
                            sem = dsem[d[1]]
                        if val <= 0:
                            continue
                        if known.get(key, 0) < val and waits.get(key, (None, 0))[1] < val:
                            waits[key] = (sem, val)
                    for key, (sem, val) in waits.items():
                        eng.wait_ge(sem, val)
                        known[key] = val
                    if op.fn is not None:
                        ins = op.fn(eng)
                        if op.sig:
                            ins.then_inc(esem[e], 1)
                        if op.dma is not None:
                            ins.then_inc(dsem[op.dma], 16)

            @block.tensor
            def _(t):
                run("pe", t)

            @block.scalar
            def _(a):
                run("act", a)

            @block.vector
            def _(v):
                run("dve", v)

            @block.gpsimd
            def _(g):
                run("pool", g)

            @block.sync
            def _(s):
                run("sp", s)


def bcast_mid(ap2d, n):
    a = ap2d.ap
    return bass.AP(ap2d.tensor, ap2d.offset, [list(a[0]), [0, n], list(a[1])])


class Builder:
    def __init__(self, layers=(0, 1, 2, 3), final_norm=True, do_mixer=True, do_ffn=True, dbg=""):
        self.dbg = dbg
        self.layers = tuple(layers)
        self.final_norm = final_norm
        self.do_mixer = do_mixer
        self.do_ffn = do_ffn
        self.nc = bass.Bass("TRN2", target_bir_lowering=False)
        self.pg = Prog(self.nc)
        self._declare_dram()
        self._persistent()

    def _in(self, name, shape, dt=F32):
        return self.nc.dram_tensor(name, list(shape), dt, kind="ExternalInput").ap()

    def _scr(self, name, shape, dt):
        kind = "ExternalOutput" if name in self.dbg.split(",") else "Internal"
        return self.nc.dram_tensor(name, list(shape), dt, kind=kind).ap()

    def _declare_dram(self):
        nc = self.nc
        self.xT = self._in("xT", [2, D, SEQ])
        self.ctxT = self._in("ctxT", [D, 2 * CTX])
        self.cT = self._in("cT", [128, KC, 3])
        self.w_ada = self._in("w_ada", [DEPTH, D, 6 * D])
        self.b_adaT = self._in("b_adaT", [128, DEPTH, 48])
        self.g1T = self._in("g1T", [128, DEPTH, KC])
        self.g2T = self._in("g2T", [128, DEPTH, KC])
        self.gfT = self._in("gfT", [128, KC])
        self.w_in = self._in("w_in_ext", [2, D, WIN_COLS])
        self.convT = self._in("convT", [128, 2, 3, 4])
        self.sinkb = self._in("sinkb", [128, 2, 8])
        self.w_mix = self._in("w_mix_out", [2, D, D])
        self.w_four = self._in("w_fourier_out", [2, D, D])
        self.w_g = self._in("w_ffn_gate", [DEPTH, D, DFF])
        self.w_u = self._in("w_ffn_up", [DEPTH, D, DFF])
        self.w_d = self._in("w_ffn_down", [DEPTH, DFF, D])
        self.cs256 = self._in("cs256", [256, 512], BF16)
        self.csp256 = self._in("csp256", [256, 512], BF16)
        self.cnf = self._in("cnf", [SEQ // 2, SEQ], BF16)
        self.snf = self._in("snf", [SEQ // 2, SEQ], BF16)
        self.ijn = self._in("ijn", [128, 3, 128], BF16)
        self.c0row = self._in("c0row", [1, NT], BF16)
        self.ropeC = self._in("ropeC", [128, SEQ])
        self.ropeS = self._in("ropeS", [128, SEQ])
        self.masks = self._in("masks", [128, 2, 512], BF16)
        self.ident = self._in("ident", [128, 128], BF16)
        self.selrow = self._in("selrow", [1, 128], BF16)
        self.yT = nc.dram_tensor("yT", [2, D, SEQ], F32, kind="ExternalOutput").ap()
        self.xs = self._scr("xs", [2, D, SEQ], F32)
        self.xcs = self._scr("xcs", [D, 2 * CTX], F32)
        self.bgs = self._scr("bgs", [2, 512, SEQ], F32)
        self.us = self._scr("us", [2, 512, SEQ], F32)
        self.qs = self._scr("qs", [2, 512, SEQ], BF16)
        self.kks = self._scr("kks", [2, 128, SEQ], BF16)
        self.vs = self._scr("vs", [2, SEQ, 256], BF16)
        self.bgc = self._scr("bgc", [512, 2 * CTX], F32)
        self.uc = self._scr("uc", [512, 2 * CTX], F32)
        self.qc = self._scr("qc", [512, 2 * CTX], BF16)
        self.kkc = self._scr("kkc", [128, 2 * CTX], BF16)
        self.vc = self._scr("vc", [2 * CTX, 256], BF16)
        self.zs = self._scr("zs", [2, SEQ, 2048], BF16)
        self.zc = self._scr("zc", [2 * CTX, 2048], BF16)
        self.fs = self._scr("fs", [2, D, SEQ], BF16)
        self.fc = self._scr("fc", [D, 2 * CTX], BF16)
        self.r_x = [RL(NTILE, f"x{b}_") for b in range(2)]
        self.r_xc = Res("xc")
        self.r_m1 = [[RL(5, f"m1_{b}_{i}_") for i in range(NTILE)] for b in range(2)]
        self.r_m1c = RL(5, "m1c")
        self.r_z = [RL(NTILE, f"z{b}_") for b in range(2)]
        self.r_zc = Res("zc")
        self.r_f = [[RL(8, f"f{b}_{i}_") for i in range(NTILE)] for b in range(2)]
        self.r_fc = RL(16, "fc")
        self.x_cur = self.xT
        self.xc_cur = self.ctxT

    def _persistent(self):
        pg = self.pg
        self.modT = pg.alloc("modT", [128, DEPTH, 48, 3], F32)
        self.A1 = pg.alloc("A1", [128, DEPTH, KC, 3], F32)
        self.A2 = pg.alloc("A2", [128, DEPTH, KC, 3], F32)
        self.scT = pg.alloc("scT", [128, KC, 3], F32)
        self.badaT_sb = pg.alloc("bada", [128, DEPTH, 48], F32)
        self.g1_sb = pg.alloc("g1", [128, DEPTH, KC], F32)
        self.g2_sb = pg.alloc("g2", [128, DEPTH, KC], F32)
        self.gf_sb = pg.alloc("gf", [128, KC], F32)
        self.ones_bf = pg.alloc("ones", [128, 128], BF16)
        self.conv_sb = pg.alloc("conv", [128, 2, 3, 4], F32)
        self.esink = pg.alloc("esink", [128, 2, 8], F32)
        self.r_const = Res("const")
        self.r_mod = Res("mod")
        pg.end_persistent()
        self.banks = [self.nc.alloc_psum_tensor(f"bank{i}", [128, 512], F32).ap() for i in range(8)]
        self.r_bank = RL(8, "bank")

    def mod_col(self, l, j0, kc, col):
        return self.modT[:, l, j0 * 8 + kc, col:col + 1]

    def load_consts(self):
        pg = self.pg
        c = self.r_const
        for i, (dst, src) in enumerate([
            (self.scT, self.cT), (self.badaT_sb, self.b_adaT), (self.g1_sb, self.g1T), (self.g2_sb, self.g2T),
            (self.gf_sb, self.gfT), (self.conv_sb, self.convT), (self.esink, self.sinkb),
        ]):
            r = Res(f"c{i}")
            pg.dma("sp", dst, src, [], [r], key=("const",))
        pg.memset("dve", self.ones_bf, 1.0 / D, [c])
        pg.barrier()
        pg.act(self.scT, self.scT, AF.Silu, [c], [c])
        pg.act(self.esink, self.esink, AF.Exp, [c], [c])
        pg.barrier()

    def phase_adaln(self):
        pg = self.pg
        pg.phase_begin()
        NP = 8
        PW = 6 * D // NP
        wst = [pg.alloc(f"wada{i}", [128, KC, PW], F32) for i in range(2)]
        wres = RL(2, "wada")
        nb = 0
        for l in self.layers:
            src = self.w_ada[l].rearrange("(kc p) f -> p kc f", p=128)
            for piece in range(NP):
                nb2 = getattr(self, '_npiece', 0)
                self._npiece = nb2 + 1
                buf = nb2 % 2
                pg.dma("sp", wst[buf], src[:, :, piece * PW:(piece + 1) * PW], [], [wres[buf]], key=("wada", buf))
                for j in range(PW // 128):
                    fch = piece * (PW // 128) + j
                    bk = nb % 4
                    nb += 1
                    ps = self.banks[bk][:, 0:3]
                    for kc in range(KC):
                        pg.mm(ps, wst[buf][:, kc, j * 128:(j + 1) * 128], self.scT[:, kc, :], kc == 0, kc == KC - 1,
                              [wres[buf], self.r_const], [self.r_bank[bk]])
                    pg.ts("dve", self.modT[:, l, fch, :], ps, self.badaT_sb[:, l, fch:fch + 1], None, ALU.add, None,
                          [self.r_bank[bk], self.r_const], [self.r_mod])
        pg.barrier()
        for l in self.layers:
            for col in range(3):
                for (A, gsb, j0) in ((self.A1, self.g1_sb, 1), (self.A2, self.g2_sb, 4)):
                    pg.ts("dve", A[:, l, :, col], self.modT[:, l, j0 * 8:(j0 + 1) * 8, col], 1.0, None, ALU.add, None,
                          [self.r_mod], [self.r_mod])
                    pg.tt("dve", A[:, l, :, col], A[:, l, :, col], gsb[:, l, :], ALU.mult, [self.r_mod, self.r_const],
                          [self.r_mod])
        pg.barrier()

    def norm_mod(self, xt, xres, nt, Afn, shfn, h, hres, scratch):
        pg = self.pg
        sq, sqres, rstd, rres, tmp, tres, ssb = scratch
        for kc in range(KC):
            s = kc % 2
            pg.act(sq[s][:, :nt], xt[:, kc, :], AF.Square, [xres[kc]], [sqres[s]])
            pg.mm(self.banks[ssb][:, :nt], self.ones_bf, sq[s][:, :nt], kc == 0, kc == KC - 1,
                  [sqres[s], self.r_const], [self.r_bank[ssb]])
        pg.act(rstd[:, :nt], self.banks[ssb][:, :nt], AF.Sqrt, [self.r_bank[ssb]], [rres], bias=EPS, scale=1.0)
        pg.op("dve", lambda v, o=rstd[:, :nt]: v.reciprocal(o, o), [rres], [rres])
        for kc in range(KC):
            s = kc % 2
            pg.tt("dve", tmp[s][:, :nt], xt[:, kc, :], rstd[:, :nt], ALU.mult, [xres[kc], rres], [tres[s]])
            sh = shfn(kc)
            pg.act(h[:, kc, :], tmp[s][:, :nt], AF.Identity, [tres[s], self.r_mod, self.r_const], [hres[kc]],
                   bias=(sh if sh is not None else 0.0), scale=Afn(kc))

    def norm_scratch(self, ssb, tmp=None, rtmp=None):
        pg = self.pg
        sq = [pg.alloc(f"sq{i}", [128, NT], BF16) for i in range(2)]
        rstd = pg.alloc("rstd", [128, NT], F32)
        if tmp is None:
            tmp = [pg.alloc(f"ntmp{i}", [128, NT], F32) for i in range(2)]
            rtmp = RL(2, "ntmp")
        return (sq, RL(2, "sq"), rstd, Res("rstd"), tmp, rtmp, ssb)

    def main_tiles(self):
        return [dict(b=b, i=i, ctx=False, col=b) for b in range(2) for i in range(NTILE)]

    def x_src(self, t):
        if t["ctx"]:
            return self.xc_cur.rearrange("(kc p) t -> p kc t", p=128), self.r_xc
        return (self.x_cur[t["b"]].rearrange("(kc p) t -> p kc t", p=128)[:, :, t["i"] * NT:(t["i"] + 1) * NT],
                self.r_x[t["b"]][t["i"]])

    def x_dst(self, t, final=False):
        if t["ctx"]:
            return self.xcs.rearrange("(kc p) t -> p kc t", p=128), self.r_xc
        base = self.yT if final else self.xs
        return (base[t["b"]].rearrange("(kc p) t -> p kc t", p=128)[:, :, t["i"] * NT:(t["i"] + 1) * NT],
                self.r_x[t["b"]][t["i"]])

    def phase_ffn(self, l, with_ctx, last):
        pg = self.pg
        pg.phase_begin()
        Wg = pg.alloc("Wg", [128, KC, DFF], BF16)
        Wu = pg.alloc("Wu", [128, KC, DFF], BF16)
        Wd = pg.alloc("Wd", [128, FC, D], BF16)
        rWg, rWu, rWd = RL(2, "Wg"), RL(2, "Wu"), RL(2, "Wd")
        gsrc = self.w_g[l].rearrange("(kc p) f -> p kc f", p=128)
        usrc = self.w_u[l].rearrange("(kc p) f -> p kc f", p=128)
        dsrc = self.w_d[l].rearrange("(fc p) n -> p fc n", p=128)
        for hlf in range(2):
            pg.dma("pool", Wg[:, 4 * hlf:4 * hlf + 4, :], gsrc[:, 4 * hlf:4 * hlf + 4, :], [], [rWg[hlf]], key=("Wg", hlf))
            pg.dma("pool", Wu[:, 4 * hlf:4 * hlf + 4, :], usrc[:, 4 * hlf:4 * hlf + 4, :], [], [rWu[hlf]], key=("Wu", hlf))
        for hlf in range(2):
            pg.dma("pool", Wd[:, 11 * hlf:11 * hlf + 11, :], dsrc[:, 11 * hlf:11 * hlf + 11, :], [], [rWd[hlf]], key=("Wd", hlf))
        xt = [pg.alloc(f"xt{i}", [128, KC, NT], F32) for i in range(2)]
        rxt = [RL(KC, f"xt{i}_") for i in range(2)]
        h = pg.alloc("h", [128, KC, NT], BF16)
        rh = RL(KC, "h")
        a = pg.alloc("a", [128, FC, NT], BF16)
        ra = RL(FC, "a")
        sg = [pg.alloc(f"sg{i}", [128, NT], F32) for i in range(2)]
        rsg = RL(2, "sg")
        scratch = self.norm_scratch(ssb=6, tmp=sg, rtmp=rsg)
        gb, ub, db = (0, 1), (2, 3), (4, 5)
        tiles = self.main_tiles()
        if with_ctx:
            tiles.append(dict(b=0, i=0, ctx=True, col=2))

        def load(idx):
            t = tiles[idx]
            src, r = self.x_src(t)
            pg.dma("sp", xt[idx % 2], src, [r], rxt[idx % 2], key=("xt", idx % 2))

        def norm(idx):
            t = tiles[idx]
            col = t["col"]
            X, rX = xt[idx % 2], rxt[idx % 2]
            self.norm_mod(X, rX, NT, lambda kc: self.A2[:, l, kc, col:col + 1], lambda kc: self.mod_col(l, 3, kc, col),
                          h, rh, scratch)

        def gateup(idx, lo=0, hi=FC):
            for fc in range(lo, hi):
                s = fc % 2
                for (W, rW, bk) in ((Wg, rWg, gb[s]), (Wu, rWu, ub[s])):
                    for kc in range(KC):
                        pg.mm(self.banks[bk], W[:, kc, fc * 128:(fc + 1) * 128], h[:, kc, :], kc == 0, kc == KC - 1,
                              [rW[kc // 4], rh[kc]], [self.r_bank[bk]])
                pg.act(sg[s], self.banks[gb[s]], AF.Silu, [self.r_bank[gb[s]]], [rsg[s]])
                pg.tt("dve", a[:, fc, :], sg[s], self.banks[ub[s]], ALU.mult, [rsg[s], self.r_bank[ub[s]]], [ra[fc]])

        def down(idx):
            t = tiles[idx]
            col = t["col"]
            X, rX = xt[idx % 2], rxt[idx % 2]
            for oc in range(KC):
                bk = db[oc % 2]
                for fc in range(FC):
                    pg.mm(self.banks[bk], Wd[:, fc, oc * 128:(oc + 1) * 128], a[:, fc, :], fc == 0, fc == FC - 1,
                          [rWd[fc // 11], ra[fc]], [self.r_bank[bk]])
                pg.stt("dve", X[:, oc, :], self.banks[bk], self.mod_col(l, 5, oc, col), X[:, oc, :], ALU.mult, ALU.add,
                       [self.r_bank[bk], rX[oc], self.r_mod], [rX[oc]])

        def fnorm(idx):
            t = tiles[idx]
            X, rX = xt[idx % 2], rxt[idx % 2]
            if last and self.final_norm and not t["ctx"]:
                self.norm_mod(X, rX, NT, lambda kc: self.gf_sb[:, kc:kc + 1], lambda kc: None, X, rX, scratch)

        def store(idx):
            t = tiles[idx]
            dst, r = self.x_dst(t, final=last)
            pg.dma("sp", dst, xt[idx % 2], rxt[idx % 2], [r], key=("xst", idx % 2))

        load(0)
        norm(0)
        if len(tiles) > 1:
            load(1)
        for idx in range(len(tiles)):
            gateup(idx, 0, FC // 2)
            if idx >= 1:
                fnorm(idx - 1)
                store(idx - 1)
                if idx + 1 < len(tiles):
                    load(idx + 1)
            gateup(idx, FC // 2, FC)
            if idx + 1 < len(tiles):
                norm(idx + 1)
            down(idx)
        fnorm(len(tiles) - 1)
        store(len(tiles) - 1)
        self.x_cur = self.xs
        if with_ctx:
            self.xc_cur = self.xcs


    def phase_m1(self, l, ctx_full):
        pg = self.pg
        e = l // 2
        pg.phase_begin()
        W = pg.alloc("Win", [128, KC, WIN_COLS], BF16)
        rW = RL(4, "Win")
        wsrc = self.w_in[e].rearrange("(kc p) f -> p kc f", p=128)
        for q4 in range(4):
            pg.dma("pool", W[:, 2 * q4:2 * q4 + 2, :], wsrc[:, 2 * q4:2 * q4 + 2, :], [], [rW[q4]], key=("Win", q4))
        xt = [pg.alloc(f"xt{i}", [128, KC, NT], F32) for i in range(2)]
        rxt = [RL(KC, f"xt{i}_") for i in range(2)]
        rcs = [pg.alloc(f"rc{i}", [128, 2, NT], F32) for i in range(2)]
        rrcs = RL(2, "rcs")
        hb = [pg.alloc(f"h{i}", [128, KC, NT], BF16) for i in range(2)]
        rhb = [RL(KC, f"h{i}_") for i in range(2)]
        scratch = self.norm_scratch(ssb=7)
        bgst = [pg.alloc(f"bgst{i}", [128, 4, NT], F32) for i in range(2)]
        ust = [pg.alloc(f"ust{i}", [128, 4, NT], F32) for i in range(2)]
        qst = [pg.alloc(f"qst{i}", [128, 4, NT], BF16) for i in range(2)]
        kkst = [pg.alloc(f"kkst{i}", [128, NT], BF16) for i in range(2)]
        vst = [pg.alloc(f"vst{i}", [128, 4, 256], BF16) for i in range(2)]
        rbg = [RL(4, "bgst") for _ in range(2)]
        rus = [RL(4, "ust") for _ in range(2)]
        rq = [RL(4, "qst") for _ in range(2)]
        rkk = [RL(1, "kkst") for _ in range(2)]
        rv = RL(2, "vst")
        ctmp = [pg.alloc(f"ctmp{i}", [128, NT], F32) for i in range(2)]
        rct = RL(2, "ctmp")
        t1 = [pg.alloc(f"t1_{i}", [128, NT], F32) for i in range(2)]
        t2 = [pg.alloc(f"t2_{i}", [128, NT], F32) for i in range(2)]
        rt1, rt2 = RL(2, "t1"), RL(2, "t2")
        for i in range(2):
            pg.memset("pool", vst[i], 1.0, [rv[i]])
        tiles = self.main_tiles() + [dict(b=0, i=0, ctx=True, col=2)]
        nbk = [0]

        def nb():
            nbk[0] += 1
            return nbk[0] % 7

        def load(idx):
            t = tiles[idx]
            s = idx % 2
            src, r = self.x_src(t)
            items = [(xt[s], src, [r], rxt[s])]
            if not t["ctx"]:
                t0 = t["i"] * NT
                items.append((rcs[s][:, 0, :], self.ropeC[:, t0:t0 + NT], [], [rrcs[s]]))
                items.append((rcs[s][:, 1, :], self.ropeS[:, t0:t0 + NT], [], []))
            pg.dma_batch("sp", items, key=("m1ld", s))

        def norm(idx):
            t = tiles[idx]
            col = t["col"]
            self.norm_mod(xt[idx % 2], rxt[idx % 2], NT, lambda kc: self.A1[:, l, kc, col:col + 1],
                          lambda kc: self.mod_col(l, 0, kc, col), hb[idx % 2], rhb[idx % 2], scratch)

        def compute(idx):
            t = tiles[idx]
            s = idx % 2
            col = t["col"]
            is_ctx = t["ctx"]
            h, rh = hb[s], rhb[s]

            def proj(cj, bk):
                for kc in range(KC):
                    pg.mm(self.banks[bk], W[:, kc, cj * 128:(cj + 1) * 128], h[:, kc, :], kc == 0, kc == KC - 1,
                          [rW[kc // 2], rh[kc]], [self.r_bank[bk]])

            def roped(cj, cjr, dst, rdst, k):
                bq = nb()
                proj(cj, bq)
                if is_ctx:
                    pg.copy("act", dst, self.banks[bq], [self.r_bank[bq]], [rdst])
                    return
                br = nb()
                proj(cjr, br)
                pg.tt("dve", t1[k % 2], self.banks[bq], rcs[s][:, 0, :], ALU.mult, [self.r_bank[bq], rrcs[s]], [rt1[k % 2]])
                pg.tt("dve", t2[k % 2], self.banks[br], rcs[s][:, 1, :], ALU.mult, [self.r_bank[br], rrcs[s]], [rt2[k % 2]])
                pg.tt("pool", dst, t1[k % 2], t2[k % 2], ALU.add, [rt1[k % 2], rt2[k % 2]], [rdst])

            full = (not is_ctx) or ctx_full
            if full:
                for c in range(4):
                    bk = nb()
                    proj(c, bk)
                    pg.copy("act", bgst[s][:, c, :], self.banks[bk], [self.r_bank[bk]], [rbg[s][c]])
                for c in range(4):
                    bc = nb()
                    proj(4 + c, bc)
                    bh = nb()
                    proj(8 + c, bh)
                    pg.copy("act", ctmp[c % 2], self.banks[bc], [self.r_bank[bc]], [rct[c % 2]])
                    pg.tt("dve", ust[s][:, c, :], ctmp[c % 2], self.banks[bh], ALU.mult, [rct[c % 2], self.r_bank[bh]],
                          [rus[s][c]])
                if idx + 1 < len(tiles):
                    norm(idx + 1)
                for c in range(4):
                    roped(12 + c, 16 + c, qst[s][:, c, :], rq[s][c], c)
            elif idx + 1 < len(tiles):
                norm(idx + 1)
            roped(20, 21, kkst[s], rkk[s][0], 0)
            bv = nb()
            for blk in range(4):
                for kc in range(KC):
                    pg.mm(self.banks[bv][:, blk * 128:(blk + 1) * 128], h[:, kc, blk * 128:(blk + 1) * 128],
                          W[:, kc, 22 * 128:23 * 128], kc == 0, kc == KC - 1, [rW[kc // 2], rh[kc]], [self.r_bank[bv]])
            vout = vst[s].rearrange("p b (g x) -> p b g x", g=2)[:, :, :, 0:64]
            vin = self.banks[bv].rearrange("p (b g x) -> p b g x", b=4, g=2)
            pg.copy("act", vout, vin, [self.r_bank[bv]], [rv[s]])

        def store(idx):
            t = tiles[idx]
            s = idx % 2
            is_ctx = t["ctx"]
            full = (not is_ctx) or ctx_full
            if is_ctx:
                dr = self.r_m1c
                dbg = self.bgc.rearrange("(c p) t -> p c t", p=128)
                du = self.uc.rearrange("(c p) t -> p c t", p=128)
                dq = self.qc.rearrange("(c p) t -> p c t", p=128)
                dkk = self.kkc
                dv = self.vc.rearrange("(blk p) f -> p blk f", p=128)
            else:
                b, i = t["b"], t["i"]
                sl = slice(i * NT, (i + 1) * NT)
                dr = self.r_m1[b][i]
                dbg = self.bgs[b].rearrange("(c p) t -> p c t", p=128)[:, :, sl]
                du = self.us[b].rearrange("(c p) t -> p c t", p=128)[:, :, sl]
                dq = self.qs[b].rearrange("(c p) t -> p c t", p=128)[:, :, sl]
                dkk = self.kks[b][:, sl]
                dv = self.vs[b].rearrange("(blk p) f -> p blk f", p=128)[:, 4 * i:4 * i + 4, :]
            items = []
            if full:
                items += [(dbg, bgst[s], rbg[s], [dr[0]]), (du, ust[s], rus[s], [dr[1]]), (dq, qst[s], rq[s], [dr[2]])]
            items += [(dkk, kkst[s], rkk[s], [dr[3]]), (dv, vst[s], [rv[s]], [dr[4]])]
            pg.dma_batch("sp", items, key=("m1st", s))

        load(0)
        norm(0)
        for idx in range(len(tiles)):
            if idx + 1 < len(tiles):
                load(idx + 1)
            compute(idx)
            store(idx)

    def phase_m2(self, l, with_ctx):
        pg = self.pg
        e = l // 2
        pg.phase_begin()
        Wc = pg.alloc("Wc", [128, 4, D], BF16)
        Wa = pg.alloc("Wa", [64, 8, D], BF16)
        rWc, rWa = Res("Wc"), Res("Wa")
        pg.dma("pool", Wc, self.w_mix[e][0:512, :].rearrange("(c p) n -> p c n", p=128), [], [rWc], key=("Wc",))
        pg.dma("pool", Wa, self.w_mix[e][512:1024, :].rearrange("(h p) n -> p h n", p=64), [], [rWa], key=("Wa",))
        self.masks_sb = pg.alloc("masks", [128, 2, 512], BF16)
        self.ident_sb = pg.alloc("ident", [128, 128], BF16)
        self.sel_sb = pg.alloc("sel", [1, 128], BF16)
        rmk = Res("maskconst")
        pg.dma_batch("sp", [(self.masks_sb, self.masks, [], [rmk]), (self.ident_sb, self.ident, [], []),
                            (self.sel_sb, self.selrow, [], [])], key=("m2const",))
        eskz = pg.alloc("eskz", [1, 2, NT], F32)
        esk = pg.alloc("esk", [1, 2, NT], BF16)
        resk = Res("esk")
        pg.memset("pool", eskz, 0.0, [resk])
        for g in range(2):
            for j in range(4):
                pg.ts("pool", esk[0:1, g, j * 128:(j + 1) * 128], eskz[0:1, g, j * 128:(j + 1) * 128],
                      self.esink[0:1, e, 4 * g + j:4 * g + j + 1], None, ALU.add, None, [resk, self.r_const], [resk])
        kctx = pg.alloc("kctx", [64, 2, 2 * CTX], BF16)
        vctx = pg.alloc("vctx", [128, 4, 256], BF16)
        rkctx, rvctx = Res("kctx"), Res("vctx")
        pg.dma_batch("sp", [
            (kctx, self.kkc.rearrange("(g p) t -> p g t", p=64), [self.r_m1c[3]], [rkctx]),
            (vctx, self.vc.rearrange("(blk p) f -> p blk f", p=128), [self.r_m1c[4]], [rvctx]),
        ], key=("ctxkv",))
        xt = [pg.alloc(f"xt{i}", [128, KC, NT], F32) for i in range(3)]
        bgt = [pg.alloc(f"bgt{i}", [128, 4, NT], F32) for i in range(2)]
        ut = [pg.alloc(f"ut{i}", [128, 4, NT + 2], F32) for i in range(2)]
        qt = [pg.alloc(f"qt{i}", [64, 8, NT], BF16) for i in range(2)]
        kkt = [pg.alloc(f"kkt{i}", [64, 2, NT + 256], BF16) for i in range(2)]
        vt = [pg.alloc(f"vt{i}", [128, 6, 256], BF16) for i in range(2)]
        rxt = [RL(KC, f"xt{i}_") for i in range(3)]
        rbgt, rut, rqt, rkkt, rvt = RL(2, "bgt"), RL(2, "ut"), RL(2, "qt"), RL(2, "kkt"), RL(2, "vt")
        pt = [pg.alloc(f"pt{i}", [128, 5, NT], BF16) for i in range(2)]
        rpt = [RL(5, f"pt{i}_") for i in range(2)]
        aout = [pg.alloc(f"aout{i}", [128, 4, NT], BF16) for i in range(2)]
        raout = [RL(4, f"aout{i}_") for i in range(2)]
        bout = [pg.alloc(f"bout{i}", [64, 8, NT], BF16) for i in range(2)]
        rbout = [RL(2, f"bout{i}_") for i in range(2)]
        ytmp = pg.alloc("ytmp", [128, 4, NT], F32)
        rytmp = RL(4, "ytmp")
        dsum = [pg.alloc(f"dsum{i}", [64, NT], F32) for i in range(2)]
        rdsum = RL(2, "dsum")
        tiles = [dict(b=b, i=i, ctx=False, col=b, nt=NT) for b in range(2) for i in range(NTILE)]
        if with_ctx:
            tiles += [dict(b=b, i=0, ctx=True, col=2, nt=CTX) for b in range(2)]
        cnt = dict(sb=0, ob=0, slot=0)

        def load_x(idx):
            t = tiles[idx]
            s3 = idx % 3
            if t["ctx"]:
                b_ = t["b"]
                xsrc = self.xc_cur.rearrange("(kc p) t -> p kc t", p=128)[:, :, b_ * CTX:(b_ + 1) * CTX]
                pg.dma("sp", xt[s3][:, :, :CTX], xsrc, [self.r_xc], rxt[s3], key=("m2x", s3))
            else:
                xsrc, rx = self.x_src(t)
                pg.dma("sp", xt[s3], xsrc, [rx], rxt[s3], key=("m2x", s3))

        def load_a(idx):
            t = tiles[idx]
            s = idx % 2
            b, i, nt = t["b"], t["i"], t["nt"]
            if t["ctx"]:
                sl = slice(b * CTX, (b + 1) * CTX)
                dr = self.r_m1c
                pg.memset("pool", ut[s], 0.0, [rut[s]])
                items = [
                    (bgt[s][:, :, :nt], self.bgc.rearrange("(c p) t -> p c t", p=128)[:, :, sl], [dr[0]], [rbgt[s]]),
                    (ut[s][:, :, 1:nt + 1], self.uc.rearrange("(c p) t -> p c t", p=128)[:, :, sl], [dr[1]], [rut[s]]),
                    (qt[s][:, :, :nt], self.qc.rearrange("(h p) t -> p h t", p=64)[:, :, sl], [dr[2]], [rqt[s]]),
                ]
            else:
                t0 = i * NT
                nbrs = [self.r_m1[b][k] for k in (i - 1, i, i + 1) if 0 <= k < NTILE]
                if i == 0 or i == NTILE - 1:
                    pg.memset("pool", ut[s], 0.0, [rut[s]])
                ulo, uhi = max(t0 - 1, 0), min(t0 + NT + 1, SEQ)
                klo, khi = max(t0 - 128, 0), min(t0 + NT + 128, SEQ)
                blo, bhi = klo // 128, khi // 128
                items = [
                    (bgt[s], self.bgs[b].rearrange("(c p) t -> p c t", p=128)[:, :, t0:t0 + NT], [self.r_m1[b][i][0]], [rbgt[s]]),
                    (ut[s][:, :, ulo - (t0 - 1):uhi - (t0 - 1)], self.us[b].rearrange("(c p) t -> p c t", p=128)[:, :, ulo:uhi],
                     [r[1] for r in nbrs], [rut[s]]),
                    (qt[s], self.qs[b].rearrange("(h p) t -> p h t", p=64)[:, :, t0:t0 + NT], [self.r_m1[b][i][2]], [rqt[s]]),
                    (kkt[s][:, :, klo - (t0 - 128):khi - (t0 - 128)], self.kks[b].rearrange("(g p) t -> p g t", p=64)[:, :, klo:khi],
                     [r[3] for r in nbrs], [rkkt[s]]),
                    (vt[s][:, blo - (4 * i - 1):bhi - (4 * i - 1), :], self.vs[b].rearrange("(blk p) f -> p blk f", p=128)[:, blo:bhi, :],
                     [r[4] for r in nbrs], [rvt[s]]),
                ]
            pg.dma_batch("sp", items, key=("m2ld", s))

        def conv(idx):
            t = tiles[idx]
            s = idx % 2
            nt = t["nt"]
            cw = lambda tap, c: self.conv_sb[:, e, tap, c:c + 1]
            for c in range(4):
                pg.act(ytmp[:, c, :nt], ut[s][:, c, 1:nt + 1], AF.Copy, [rut[s], self.r_const], [rytmp[c]], scale=cw(1, c))
            for c in range(4):
                pg.stt("dve", ytmp[:, c, :nt], ut[s][:, c, 0:nt], cw(0, c), ytmp[:, c, :nt], ALU.mult, ALU.add,
                       [rut[s], self.r_const, rytmp[c]], [rytmp[c]])
            for c in range(4):
                pg.stt("dve", ytmp[:, c, :nt], ut[s][:, c, 2:nt + 2], cw(2, c), ytmp[:, c, :nt], ALU.mult, ALU.add,
                       [rut[s], self.r_const, rytmp[c]], [rytmp[c]])
            for c in range(4):
                pg.tt("pool", aout[s][:, c, :nt], bgt[s][:, c, :nt], ytmp[:, c, :nt], ALU.mult, [rbgt[s], rytmp[c]],
                      [raout[s][c]])

        def make(n):
            idx, qb, g = groups[n]
            t = tiles[idx]
            s = idx % 2
            b, i, is_ctx = t["b"], t["i"], t["ctx"]
            ents = []
            if not is_ctx:
                G = 4 * i + qb
                if G > 0:
                    ents.append((kkt[s][:, g, qb * 128:(qb + 1) * 128],
                                 vt[s][:, qb, g * 128:(g + 1) * 128], 0, rkkt[s], rvt[s]))
                ents.append((kkt[s][:, g, (qb + 1) * 128:(qb + 2) * 128],
                             vt[s][:, qb + 1, g * 128:(g + 1) * 128], None, rkkt[s], rvt[s]))
                if G < SEQ // 128 - 1:
                    ents.append((kkt[s][:, g, (qb + 2) * 128:(qb + 3) * 128],
                                 vt[s][:, qb + 2, g * 128:(g + 1) * 128], 1, rkkt[s], rvt[s]))
            for j2 in range(2):
                ents.append((kctx[:, g, b * CTX + j2 * 128:b * CTX + (j2 + 1) * 128],
                             vctx[:, 2 * b + j2, g * 128:(g + 1) * 128], None, rkctx, rvctx))
            return dict(idx=idx, qb=qb, g=g, s=s, ents=ents, slot=n % 2, n=n)

        SB = (0, 1, 2)

        def scores(grp):
            s, qb, g, slot = grp["s"], grp["qb"], grp["g"], grp["slot"]
            for kbi, (kap, vap, mk, rk, rvv) in enumerate(grp["ents"]):
                bk = SB[cnt["sb"] % len(SB)]
                cnt["sb"] += 1
                pg.mm(self.banks[bk].rearrange("p (j q) -> p j q", j=4), kap,
                      qt[s][:, 4 * g:4 * g + 4, qb * 128:(qb + 1) * 128], True, mk is None,
                      [rk, rqt[s]], [self.r_bank[bk]])
                if mk is not None:
                    pg.mm(self.banks[bk], self.ident_sb, self.masks_sb[:, mk, :], False, True,
                          [rmk], [self.r_bank[bk]])
                pg.act(pt[slot][:, kbi, :], self.banks[bk], AF.Exp, [self.r_bank[bk]], [rpt[slot][kbi]], scale=0.125)

        def pv_epi(grp):
            s, qb, g, slot, n = grp["s"], grp["qb"], grp["g"], grp["slot"], grp["n"]
            ents = grp["ents"]
            ob = 3 + n % 2
            ds = n % 2
            for kbi, (kap, vap, mk, rk, rvv) in enumerate(ents):
                pg.mm(self.banks[ob], vap, pt[slot][:, kbi, :], kbi == 0, False,
                      [rvv, rpt[slot][kbi]], [self.r_bank[ob]])
            pg.mm(self.banks[ob], self.sel_sb[0:1, :], esk[0:1, g, :], False, True, [rmk, resk], [self.r_bank[ob]])
            pg.act(dsum[ds], self.banks[ob][64:128, :], AF.Ln, [self.r_bank[ob]], [rdsum[ds]])
            pg.act(dsum[ds], dsum[ds], AF.Exp, [rdsum[ds]], [rdsum[ds]], scale=-1.0)
            pg.tt("dve", bout[s][:, 4 * g:4 * g + 4, qb * 128:(qb + 1) * 128],
                  self.banks[ob][0:64, :].rearrange("p (j q) -> p j q", j=4),
                  dsum[ds].rearrange("p (j q) -> p j q", j=4), ALU.mult,
                  [self.r_bank[ob], rdsum[ds]], [rbout[s][g]])

        def mix_part(idx, oc):
            t = tiles[idx]
            s = idx % 2
            nt, col = t["nt"], t["col"]
            X, rX = xt[idx % 3], rxt[idx % 3]
            bk = 5 + oc % 2
            for c in range(4):
                pg.mm(self.banks[bk][:, :nt], Wc[:, c, oc * 128:(oc + 1) * 128], aout[s][:, c, :nt], c == 0, False,
                      [rWc, raout[s][c]], [self.r_bank[bk]])
            for hh in range(8):
                pg.mm(self.banks[bk][:, :nt], Wa[:, hh, oc * 128:(oc + 1) * 128], bout[s][:, hh, :nt], False, hh == 7,
                      [rWa, rbout[s][hh // 4]], [self.r_bank[bk]])
            pg.stt("dve", X[:, oc, :nt], self.banks[bk][:, :nt], self.mod_col(l, 2, oc, col), X[:, oc, :nt],
                   ALU.mult, ALU.add, [self.r_bank[bk], rX[oc], self.r_mod], [rX[oc]])

        def store(idx):
            t = tiles[idx]
            s3 = idx % 3
            if t["ctx"]:
                b_ = t["b"]
                dst = self.xcs.rearrange("(kc p) t -> p kc t", p=128)[:, :, b_ * CTX:(b_ + 1) * CTX]
                pg.dma("sp", dst, xt[s3][:, :, :CTX], rxt[s3], [self.r_xc], key=("xst", s3))
            else:
                dst, r_ = self.x_dst(t)
                pg.dma("sp", dst, xt[s3], rxt[s3], [r_], key=("xst", s3))

        groups = [(idx, qb, g) for idx, t in enumerate(tiles) for qb in range(t["nt"] // 128) for g in range(2)]
        ngrp = {idx: 2 * (t["nt"] // 128) for idx, t in enumerate(tiles)}
        for i_ in range(min(2, len(tiles))):
            load_x(i_)
            load_a(i_)
        conv(0)
        cur = make(0)
        scores(cur)
        kpos = 0
        pending = []
        for n in range(len(groups)):
            idx = groups[n][0]
            ng = ngrp[idx]
            new_tile_next = (n + 1 < len(groups) and groups[n + 1][0] != idx)
            nxt = None
            if n + 1 < len(groups):
                if new_tile_next:
                    conv(groups[n + 1][0])
                nxt = make(n + 1)
                scores(nxt)
            pv_epi(cur)
            if idx >= 1 and pending:
                steps_left = (ng - 1) - kpos
                take = -(-len(pending) // max(steps_left, 1)) if steps_left > 0 else len(pending)
                for _ in range(take):
                    mix_part(idx - 1, pending.pop(0))
                if not pending:
                    store(idx - 1)
                    if idx + 1 < len(tiles) and idx + 1 >= 2:
                        load_x(idx + 1)
            kpos += 1
            if kpos == ng:
                kpos = 0
                pending = list(range(KC))
                if idx + 2 < len(tiles):
                    load_a(idx + 2)
            cur = nxt
        last = len(tiles) - 1
        for oc in range(KC):
            mix_part(last, oc)
        store(last)
        self.x_cur = self.xs
        if with_ctx:
            self.xc_cur = self.xcs

    def phase_f1(self, l, with_ctx):
        pg = self.pg
        pg.phase_begin()
        cs = pg.alloc("cs", [128, 2, 512], BF16)
        rcs = Res("cs")
        pg.dma("sp", cs, self.cs256.rearrange("(kc p) f -> p kc f", p=128), [], [rcs], key=("cs",))
        xt = [pg.alloc(f"xt{i}", [128, KC, NT], F32) for i in range(2)]
        rxt = [RL(KC, f"xt{i}_") for i in range(2)]
        hb = [pg.alloc(f"h{i}", [128, KC, NT], BF16) for i in range(2)]
        rhb = [RL(KC, f"h{i}_") for i in range(2)]
        scratch = self.norm_scratch(ssb=7)
        zst = [pg.alloc(f"zst{i}", [128, 4, 2048], BF16) for i in range(2)]
        rzst = [RL(8, f"zst{i}_") for i in range(2)]
        tiles = self.main_tiles()
        if with_ctx:
            tiles.append(dict(b=0, i=0, ctx=True, col=2))
        nbk = [0]

        def load(idx):
            src, r = self.x_src(tiles[idx])
            pg.dma("sp", xt[idx % 2], src, [r], rxt[idx % 2], key=("xt", idx % 2))

        def norm(idx):
            col = tiles[idx]["col"]
            self.norm_mod(xt[idx % 2], rxt[idx % 2], NT, lambda kc: self.A1[:, l, kc, col:col + 1],
                          lambda kc: self.mod_col(l, 0, kc, col), hb[idx % 2], rhb[idx % 2], scratch)

        def compute(idx):
            t = tiles[idx]
            s = idx % 2
            h, rh = hb[s], rhb[s]
            for blk in range(4):
                if blk == 1 and idx + 1 < len(tiles):
                    norm(idx + 1)
                for g in range(4):
                    bk = nbk[0] % 7
                    nbk[0] += 1
                    for k2 in range(2):
                        pg.mm(self.banks[bk], h[:, 2 * g + k2, blk * 128:(blk + 1) * 128], cs[:, k2, :], k2 == 0, k2 == 1,
                              [rh[2 * g + k2], rcs], [self.r_bank[bk]])
                    zv = zst[s][:, blk, g * 512:(g + 1) * 512].rearrange("p (hf c k) -> p hf c k", hf=2, c=2)
                    pv = self.banks[bk].rearrange("p (c hf k) -> p hf c k", c=2, hf=2)
                    pg.copy("act" if g % 2 == 0 else "dve", zv, pv, [self.r_bank[bk]], [rzst[s][2 * blk + g % 2]])

        def store(idx):
            t = tiles[idx]
            s = idx % 2
            if t["ctx"]:
                dst, r = self.zc.rearrange("(blk p) f -> p blk f", p=128), self.r_zc
            else:
                b, i = t["b"], t["i"]
                dst = self.zs[b].rearrange("(blk p) f -> p blk f", p=128)[:, 4 * i:4 * i + 4, :]
                r = self.r_z[b][i]
            pg.dma("sp", dst, zst[s], rzst[s], [r], key=("zst", s))

        load(0)
        norm(0)
        for idx in range(len(tiles)):
            if idx + 1 < len(tiles):
                load(idx + 1)
            compute(idx)
            store(idx)

    def phase_f2(self, l, with_ctx):
        pg = self.pg
        pg.phase_begin()
        NF = 16
        Zf = pg.alloc("Zf", [128, NF, 2048], BF16)
        rZf = [RL(4, f"Zf{i}_") for i in range(NF)]
        Za = [pg.alloc(f"Za{i}", [128, 2, 2048], BF16) for i in range(2)]
        Zb = [pg.alloc(f"Zb{i}", [128, 2, 2048], BF16) for i in range(2)]
        rZa, rZb = RL(2, "Za"), RL(2, "Zb")
        Z0 = pg.alloc("Z0", [1, 2048], BF16)
        rZ0 = Res("Z0")
        Tb = [pg.alloc(f"Tb{i}", [128, 2, NF, NT], BF16) for i in range(2)]
        rTb = RL(2, "Tb")
        fst = [pg.alloc(f"fst{i}", [128, NT], BF16) for i in range(4)]
        rfst = RL(4, "fst")
        ijn = pg.alloc("ijn", [128, 3, 128], BF16)
        c0 = pg.alloc("c0", [1, NT], BF16)
        cs = pg.alloc("cs", [128, 2, 512], BF16)
        rcs = Res("cs")
        rij = Res("ij")
        pg.dma_batch("sp", [(cs, self.csp256.rearrange("(kc p) f -> p kc f", p=128), [], [rcs]),
                            (ijn, self.ijn, [], [rij]), (c0, self.c0row, [], [])], key=("cs",))
        cnsrc = self.cnf.rearrange("(nc p) k -> p nc k", p=128)
        snsrc = self.snf.rearrange("(nc p) k -> p nc k", p=128)
        nev = 0
        npiece = 0
        nfold = 0
        for b in range(2):
            zrows = self.zs[b]
            pg.dma("sp", Z0, zrows[0:1, :], self.r_z[b], [rZ0], key=("Z0",))
            for pr in range(NF // 2):
                sl = pr % 2
                a_src = zrows[1 + 256 * pr:1 + 256 * pr + 256, :].rearrange("(c p) f -> p c f", p=128)
                items = [(Za[sl], a_src, self.r_z[b], [rZa[sl]])]
                for q in range(2):
                    nc_ = 2 * pr + q
                    lo = SEQ - 128 - 128 * nc_
                    items.append((Zb[sl][:, q, :], zrows[lo:lo + 128, :], self.r_z[b], [rZb[sl]] if q == 0 else []))
                pg.dma_batch("sp", items, key=("Zab", sl))
                for q in range(2):
                    nc_ = 2 * pr + q
                    for g4 in range(4):
                        bk = nfold % 7
                        nfold += 1
                        cols = slice(g4 * 512, (g4 + 1) * 512)
                        pg.mm(self.banks[bk], ijn[:, 0, :], Za[sl][:, q, cols], True, False, [rij, rZa[sl]], [self.r_bank[bk]])
                        for blk in range(4):
                            which = 1 if blk % 2 == 0 else 2
                            pg.mm(self.banks[bk][:, blk * 128:(blk + 1) * 128], ijn[:, which, :],
                                  Zb[sl][:, q, g4 * 512 + blk * 128:g4 * 512 + (blk + 1) * 128], False, blk == 3,
                                  [rij, rZb[sl]], [self.r_bank[bk]])
                        pg.copy("act" if nfold % 2 == 0 else "dve", Zf[:, nc_, cols], self.banks[bk], [self.r_bank[bk]], [rZf[nc_][g4]])
            for j in range(NTILE):
                tb = npiece % 2
                npiece += 1
                pg.dma_batch("sp", [
                    (Tb[tb][:, 0, 0:8, :], cnsrc[:, 0:8, j * NT:(j + 1) * NT], [], [rTb[tb]]),
                    (Tb[tb][:, 0, 8:16, :], cnsrc[:, 8:16, j * NT:(j + 1) * NT], [], []),
                    (Tb[tb][:, 1, 0:8, :], snsrc[:, 0:8, j * NT:(j + 1) * NT], [], []),
                    (Tb[tb][:, 1, 8:16, :], snsrc[:, 8:16, j * NT:(j + 1) * NT], [], []),
                ], key=("Tb", tb))
                for fg in range(2):
                    for c2 in range(2):
                        for m in range(4):
                            bk = 4 * fg + m
                            base = fg * 1024 + m * 256 + c2 * 128
                            for nch in range(NF):
                                pg.mm(self.banks[bk], Zf[:, nch, base:base + 128], Tb[tb][:, c2, nch, :],
                                      c2 == 0 and nch == 0, False, [rZf[nch][base // 512], rTb[tb]], [self.r_bank[bk]])
                    for m in range(4):
                        bk = 4 * fg + m
                        base = fg * 1024 + m * 256
                        pg.mm(self.banks[bk], Z0[0:1, base:base + 128], c0[0:1, :], False, True, [rZ0, rcs], [self.r_bank[bk]])
                    for m in range(4):
                        bk = 4 * fg + m
                        k = nev % 4
                        nev += 1
                        fcn = 4 * fg + m
                        pg.copy("act" if m % 2 == 0 else "dve", fst[k], self.banks[bk], [self.r_bank[bk]], [rfst[k]])
                        pg.dma("sp", self.fs[b][fcn * 128:(fcn + 1) * 128, j * NT:(j + 1) * NT], fst[k], [rfst[k]],
                               [self.r_f[b][j][fcn]], key=("fst", k))
        if with_ctx:
            Zc = pg.alloc("Zc", [128, 4, 2048], BF16)
            rZc = Res("Zc")
            pg.dma("sp", Zc, self.zc.rearrange("(blk p) f -> p blk f", p=128), [self.r_zc], [rZc], key=("Zc",))
            for b in range(2):
                for fcn in range(8):
                    bk = fcn
                    fg, m = fcn // 4, fcn % 4
                    for c2 in range(2):
                        for nch in range(2):
                            pg.mm(self.banks[bk][:, 0:CTX], Zc[:, 2 * b + nch, fg * 1024 + m * 256 + c2 * 128:fg * 1024 + m * 256 + (c2 + 1) * 128],
                                  cs[:, nch, c2 * 256:(c2 + 1) * 256], c2 == 0 and nch == 0, c2 == 1 and nch == 1,
                                  [rZc, rcs], [self.r_bank[bk]])
                    k = nev % 4
                    nev += 1
                    pg.copy("act" if fcn % 2 == 0 else "dve", fst[k][:, 0:CTX], self.banks[bk][:, 0:CTX], [self.r_bank[bk]], [rfst[k]])
                    pg.dma("sp", self.fc[fcn * 128:(fcn + 1) * 128, b * CTX:(b + 1) * CTX], fst[k][:, 0:CTX], [rfst[k]],
                           [self.r_fc[b * 8 + fcn]], key=("fst", k))

    def phase_f3(self, l, with_ctx):
        pg = self.pg
        o = l // 2
        pg.phase_begin()
        Wf = pg.alloc("Wf", [128, KC, D], BF16)
        rWf = RL(2, "Wf")
        wsrc = self.w_four[o].rearrange("(kc p) n -> p kc n", p=128)
        for hlf in range(2):
            pg.dma("pool", Wf[:, 4 * hlf:4 * hlf + 4, :], wsrc[:, 4 * hlf:4 * hlf + 4, :], [], [rWf[hlf]], key=("Wf", hlf))
        xt = [pg.alloc(f"xt{i}", [128, KC, NT], F32) for i in range(2)]
        rxt = [RL(KC, f"xt{i}_") for i in range(2)]
        ft = [pg.alloc(f"ft{i}", [128, KC, NT], BF16) for i in range(2)]
        rft = RL(2, "ft")
        tiles = self.main_tiles()
        if with_ctx:
            tiles.append(dict(b=0, i=0, ctx=True, col=2))

        def load(idx):
            t = tiles[idx]
            s = idx % 2
            src, r = self.x_src(t)
            if t["ctx"]:
                fsrc, fr = self.fc.rearrange("(kc p) t -> p kc t", p=128), self.r_fc
            else:
                b, i = t["b"], t["i"]
                fsrc = self.fs[b].rearrange("(kc p) t -> p kc t", p=128)[:, :, i * NT:(i + 1) * NT]
                fr = self.r_f[b][i]
            pg.dma_batch("sp", [(xt[s], src, [r], rxt[s]), (ft[s], fsrc, fr, [rft[s]])], key=("f3ld", s))

        def compute(idx):
            t = tiles[idx]
            s = idx % 2
            col = t["col"]
            for oc in range(KC):
                bk = oc % 4
                for kc in range(KC):
                    pg.mm(self.banks[bk], Wf[:, kc, oc * 128:(oc + 1) * 128], ft[s][:, kc, :], kc == 0, kc == KC - 1,
                          [rWf[kc // 4], rft[s]], [self.r_bank[bk]])
                pg.stt("dve", xt[s][:, oc, :], self.banks[bk], self.mod_col(l, 2, oc, col), xt[s][:, oc, :], ALU.mult, ALU.add,
                       [self.r_bank[bk], rxt[s][oc], self.r_mod], [rxt[s][oc]])

        def store(idx):
            dst, r = self.x_dst(tiles[idx])
            pg.dma("sp", dst, xt[idx % 2], rxt[idx % 2], [r], key=("xst", idx % 2))

        load(0)
        for idx in range(len(tiles)):
            if idx + 1 < len(tiles):
                load(idx + 1)
            compute(idx)
            store(idx)
        self.x_cur = self.xs
        if with_ctx:
            self.xc_cur = self.xcs

    def build(self):
        self.load_consts()
        self.phase_adaln()
        for l in self.layers:
            ctx_after = any(j % 2 == 0 for j in range(l + 1, DEPTH))
            last = (l == self.layers[-1])
            if self.do_mixer:
                if l % 2 == 0:
                    self.phase_m1(l, ctx_full=ctx_after)
                    if "nom2" not in self.dbg:
                        self.phase_m2(l, with_ctx=ctx_after)
                else:
                    self.phase_f1(l, with_ctx=ctx_after)
                    if "nof2" not in self.dbg:
                        self.phase_f2(l, with_ctx=ctx_after)
                    if "nof3" not in self.dbg:
                        self.phase_f3(l, with_ctx=ctx_after)
            if self.do_ffn:
                self.phase_ffn(l, with_ctx=ctx_after, last=last)
        self.pg.barrier()
        self.pg.emit()
        return self.nc


def _bf16(a):
    return np.ascontiguousarray(a.astype(ml_dtypes.bfloat16))


_CONST_CACHE = {}


def _constants():
    if _CONST_CACHE:
        return _CONST_CACHE
    n = np.arange(256, dtype=np.int64)
    ang = 2.0 * np.pi * ((n[:, None] * n[None, :]) % 256) / 256.0
    cs256 = np.concatenate([np.cos(ang), -np.sin(ang)], axis=1) / 16.0
    csp256 = np.concatenate([np.cos(ang), np.sin(ang)], axis=1) / 16.0
    m = np.arange(SEQ, dtype=np.int64)
    nn = np.arange(1, SEQ // 2 + 1, dtype=np.int64)
    angN = 2.0 * np.pi * ((nn[:, None] * m[None, :]) % SEQ) / SEQ
    cn = np.cos(angN) / 64.0
    sn = np.sin(angN) / 64.0
    cn[-1] *= 0.5
    sn[-1] = 0.0
    eye = np.eye(128, dtype=np.float32)
    ijn = np.stack([eye, eye[::-1], -eye[::-1]], axis=1)
    c0row = np.full((1, NT), 1.0 / 64.0, np.float32)
    t = np.arange(SEQ)
    r = (t // 64).astype(np.float32)
    col = (t % 64).astype(np.float32)
    inv = (np.float32(10000.0) ** (-np.arange(16, dtype=np.float32) / np.float32(16))).astype(np.float32)
    ropeC = np.zeros((128, SEQ), np.float32)
    ropeS = np.zeros((128, SEQ), np.float32)
    for p in range(128):
        d = p % 64
        pos = r if d < 32 else col
        a = (pos * inv[d % 16]).astype(np.float32)
        sign = -1.0 if (d % 32) < 16 else 1.0
        ropeC[p] = np.cos(a)
        ropeS[p] = sign * np.sin(a)
    kj = np.arange(128)[:, None]
    qi = np.arange(128)[None, :]
    valid = np.stack([(qi <= kj), (kj <= qi)], axis=1)
    masks = np.where(valid, 0.0, -30000.0).astype(np.float32)
    masks = np.ascontiguousarray(np.tile(masks, (1, 1, 4)))
    ident = np.eye(128, dtype=np.float32)
    selrow = np.concatenate([np.zeros((1, 64), np.float32), np.ones((1, 64), np.float32)], axis=1)
    _CONST_CACHE.update(
        cs256=_bf16(cs256.astype(np.float32)), csp256=_bf16(csp256.astype(np.float32)), cnf=_bf16(cn.astype(np.float32)), snf=_bf16(sn.astype(np.float32)),
        ijn=_bf16(ijn), c0row=_bf16(c0row),
        ropeC=ropeC, ropeS=ropeS, masks=_bf16(masks), ident=_bf16(ident), selrow=_bf16(selrow))
    return _CONST_CACHE


def _win_ext(w_in):
    Q0, K0, V0 = 1536, 2048, 2176
    partner = np.array([d + 16 if (d % 32) < 16 else d - 16 for d in range(64)])
    cols = list(range(0, 1536))
    cols += list(range(Q0, Q0 + 512))
    for hh in range(8):
        cols += list(Q0 + hh * 64 + partner)
    cols += list(range(K0, K0 + 128))
    for g in range(2):
        cols += list(K0 + g * 64 + partner)
    cols += list(range(V0, V0 + 128))
    cols = np.array(cols)
    assert cols.shape[0] == WIN_COLS
    return np.ascontiguousarray(w_in[:, :, cols])


def _shared_inputs(inp):
    f = np.float32
    sh = {}
    sh["w_ada"] = np.ascontiguousarray(inp["w_ada"], dtype=f)
    sh["b_adaT"] = np.ascontiguousarray(inp["b_ada"].reshape(DEPTH, 48, 128).transpose(2, 0, 1), dtype=f)
    sh["g1T"] = np.ascontiguousarray(inp["norm1_g"].reshape(DEPTH, KC, 128).transpose(2, 0, 1), dtype=f)
    sh["g2T"] = np.ascontiguousarray(inp["norm2_g"].reshape(DEPTH, KC, 128).transpose(2, 0, 1), dtype=f)
    sh["gfT"] = np.ascontiguousarray(inp["final_g"].reshape(KC, 128).T, dtype=f)
    sh["w_in_ext"] = _win_ext(np.asarray(inp["w_in"], dtype=f))
    sh["convT"] = np.ascontiguousarray(inp["conv_w"].reshape(2, 3, 4, 128).transpose(3, 0, 1, 2), dtype=f)
    sh["sinkb"] = np.ascontiguousarray(np.broadcast_to(inp["sink"][None], (128, 2, 8)), dtype=f)
    sh["w_mix_out"] = np.ascontiguousarray(inp["w_mix_out"], dtype=f)
    sh["w_fourier_out"] = np.ascontiguousarray(inp["w_fourier_out"], dtype=f)
    sh["w_ffn_gate"] = np.ascontiguousarray(inp["w_ffn_gate"], dtype=f)
    sh["w_ffn_up"] = np.ascontiguousarray(inp["w_ffn_up"], dtype=f)
    sh["w_ffn_down"] = np.ascontiguousarray(inp["w_ffn_down"], dtype=f)
    sh.update(_constants())
    return sh


def _core_inputs(inp, i):
    f = np.float32
    b0 = 2 * i
    m = {}
    m["xT"] = np.ascontiguousarray(np.asarray(inp["x"][b0:b0 + 2], dtype=f).transpose(0, 2, 1))
    m["ctxT"] = np.ascontiguousarray(np.asarray(inp["ctx"][b0:b0 + 2], dtype=f).transpose(2, 0, 1).reshape(D, 2 * CTX))
    cc = np.stack([inp["c"][b0], inp["c"][b0 + 1], inp["c_ctx"]], axis=0).astype(f)
    m["cT"] = np.ascontiguousarray(cc.reshape(3, KC, 128).transpose(2, 1, 0))
    return m


_NC_CACHE = {}


def _get_nc(**kw):
    key = tuple(sorted(kw.items()))
    if key not in _NC_CACHE:
        _NC_CACHE[key] = Builder(**kw).build()
    return _NC_CACHE[key]


def run_device(inputs, n_cores=N_CORES, **kw):
    nc = _get_nc(**kw)
    shared = _shared_inputs(inputs)
    in_maps = []
    for i in range(n_cores):
        m = dict(shared)
        m.update(_core_inputs(inputs, i))
        in_maps.append(m)
    res = run_bass_kernel_spmd(nc, in_maps, core_ids=list(range(n_cores)))
    if kw.get("dbg"):
        return [{k: np.asarray(v) for k, v in r.items()} for r in res.results]
    outs = [np.asarray(r["yT"]) for r in res.results]
    return outs


def kernel(**inputs):
    outs = run_device(inputs)
    y = np.concatenate([o.transpose(0, 2, 1) for o in outs], axis=0)
    return np.ascontiguousarray(y.astype(np.float32))
```
